# Optimizing a Trainium2 kernel written in Bass

```python
import math
import jax, jax.numpy as jnp
from jax import lax
import numpy as np

D_MODEL = 1024
BATCH = 32
SEQ = 2048
DEPTH = 4

N_A_LAYERS = max(1, DEPTH // 2)
N_B_LAYERS = DEPTH - N_A_LAYERS
POOL_WINDOWS = (2, 4, 8, 16)
N_POOL_GROUPS = len(POOL_WINDOWS)
POOL_GROUP_DIM = D_MODEL // N_POOL_GROUPS
HEAD_DIM = 64
N_HEADS = D_MODEL // HEAD_DIM
DILATED_GROUPS = ((128, 1), (512, 4), (2048, 16))
N_GROUPS = len(DILATED_GROUPS)
ATTN_DIM = N_HEADS * HEAD_DIM
Q_DIM = N_GROUPS * ATTN_DIM
ROPE_THETA = 10000.0
D_FF = 2816
CONV_WIDTH = 3
DEEPNORM_ALPHA = (2.0 * DEPTH) ** 0.25
DEEPNORM_BETA = (8.0 * DEPTH) ** -0.25
LN_EPS = 1e-5

kernel_name = "yoco_pool_dilated_attn_convffn_deepnorm"


def layer_norm(x, g, b):
    xf = x.astype(jnp.float32)
    mu = xf.mean(-1, keepdims=True)
    var = jnp.square(xf - mu).mean(-1, keepdims=True)
    y = (xf - mu) * lax.rsqrt(var + LN_EPS) * g.astype(jnp.float32) + b.astype(jnp.float32)
    return y.astype(x.dtype)


def rope_tables(seq):
    inv_freq = ROPE_THETA ** (-jnp.arange(0, HEAD_DIM, 2, dtype=jnp.float32) / HEAD_DIM)
    ang = jnp.arange(seq, dtype=jnp.float32)[:, None] * inv_freq[None, :]
    return jnp.cos(ang), jnp.sin(ang)


def apply_rope(t, cos, sin):
    tf = t.astype(jnp.float32)
    x1, x2 = tf[..., : HEAD_DIM // 2], tf[..., HEAD_DIM // 2:]
    c, s = cos[None, :, None, :], sin[None, :, None, :]
    return jnp.concatenate([x1 * c - x2 * s, x2 * c + x1 * s], axis=-1).astype(t.dtype)


def pool_mixer(x, pool_w, pool_scale):
    B, S, D = x.shape
    xg = x.reshape(B, S, N_POOL_GROUPS, POOL_GROUP_DIM)
    csum = jnp.cumsum(xg.astype(jnp.float32), axis=1)
    c0 = jnp.concatenate([jnp.zeros_like(csum[:, :1]), csum], axis=1)
    pos = jnp.arange(S, dtype=jnp.float32)
    pooled = []
    for g, w in enumerate(POOL_WINDOWS):
        w_eff = min(w, S)
        lagged = jnp.concatenate([jnp.zeros_like(c0[:, : w_eff - 1, g]), c0[:, : S - w_eff + 1, g]], axis=1)
        count = jnp.minimum(pos + 1.0, float(w))[None, :, None]
        pooled.append((c0[:, 1:, g] - lagged) / count)
    pooled = jnp.stack(pooled, axis=2).astype(x.dtype) - xg
    y = jnp.einsum('bsgc,gce->bsge', pooled, pool_w).reshape(B, S, D)
    return y * pool_scale


def dilated_branch(q, k, v, dilation, span):
    B, S, H, Dh = q.shape
    L = S // dilation
    nb = -(-L // span)
    pad = nb * span - L

    def strided_blocks(t):
        t = t.reshape(B, L, dilation, H, Dh).transpose(0, 2, 3, 1, 4)
        t = jnp.pad(t, ((0, 0), (0, 0), (0, 0), (0, pad), (0, 0)))
        return t.reshape(B, dilation, H, nb, span, Dh)

    def with_prev(t):
        prev = jnp.concatenate([jnp.zeros_like(t[:, :, :, :1]), t[:, :, :, :-1]], axis=3)
        return jnp.concatenate([prev, t], axis=4)

    qb = strided_blocks(q)
    kk = with_prev(strided_blocks(k))
    vv = with_prev(strided_blocks(v))
    s = jnp.einsum('brhnqc,brhnkc->brhnqk', qb, kk, preferred_element_type=jnp.float32)
    qi = jnp.arange(span)[:, None]
    kj = jnp.arange(2 * span)[None, :]
    rel = span + qi - kj
    band = (rel >= 0) & (rel <= span)
    has_prev = (jnp.arange(nb) > 0)[:, None, None] | (kj >= span)[None]
    valid = band[None] & has_prev
    s = jnp.where(valid, s, -jnp.inf)
    m = s.max(-1, keepdims=True)
    p = jnp.exp(s - m)
    l = p.sum(-1, keepdims=True)
    o = jnp.einsum('brhnqk,brhnkc->brhnqc', (p / l).astype(v.dtype), vv)
    lse = (m + jnp.log(l))[..., 0]
    o = o.reshape(B, dilation, H, nb * span, Dh)[:, :, :, :L].transpose(0, 3, 1, 2, 4).reshape(B, S, H, Dh)
    lse = lse.reshape(B, dilation, H, nb * span)[..., :L].transpose(0, 3, 1, 2).reshape(B, S, H)
    return o, lse


def dilated_attention(x, k_shared, v_shared, w_q, w_o, cos, sin):
    B, S, _ = x.shape
    q = (x @ w_q).reshape(B, S, N_GROUPS * N_HEADS, HEAD_DIM)
    q = (apply_rope(q, cos, sin) * (HEAD_DIM ** -0.5)).reshape(B, S, N_GROUPS, N_HEADS, HEAD_DIM)
    outs, lses = [], []
    for g, (window, dilation) in enumerate(DILATED_GROUPS):
        o, lse = dilated_branch(q[:, :, g], k_shared[:, :, g], v_shared[:, :, g], dilation, window // dilation)
        outs.append(o)
        lses.append(lse)
    weights = jax.nn.softmax(jnp.stack(lses, axis=0), axis=0)
    o = jnp.sum(weights[..., None].astype(x.dtype) * jnp.stack(outs, axis=0), axis=0)
    return o.reshape(B, S, ATTN_DIM) @ w_o


def conv_ffn(x, w_gate, w_up, conv_w, conv_b, w_down):
    S = x.shape[1]
    g = x @ w_gate
    u = x @ w_up
    gp = jnp.pad(g, ((0, 0), (CONV_WIDTH - 1, 0), (0, 0)))
    conv = conv_b
    for j in range(CONV_WIDTH):
        conv = conv + conv_w[j] * gp[:, j: j + S]
    h = jax.nn.gelu(conv) * u
    return h @ w_down


def setup_inputs(seed: int = 0) -> dict:
    key = jax.random.key(seed)
    ks = jax.random.split(key, 16)
    f32 = jnp.float32
    beta = DEEPNORM_BETA
    nrm = lambda k, shape, scale: jax.random.normal(k, shape, f32) * scale
    x = jax.random.normal(ks[0], (BATCH, SEQ, D_MODEL), f32)
    pool_w = nrm(ks[1], (N_A_LAYERS, N_POOL_GROUPS, POOL_GROUP_DIM, POOL_GROUP_DIM), beta * POOL_GROUP_DIM ** -0.5)
    pool_scale = 1.0 + nrm(ks[2], (N_A_LAYERS, D_MODEL), 0.1)
    w_q = nrm(ks[3], (N_B_LAYERS, D_MODEL, Q_DIM), D_MODEL ** -0.5)
    w_k = nrm(ks[4], (D_MODEL, Q_DIM), D_MODEL ** -0.5)
    w_v = nrm(ks[5], (D_MODEL, Q_DIM), beta * D_MODEL ** -0.5)
    w_kv = jnp.concatenate([w_k, w_v], axis=1)
    w_o = nrm(ks[6], (N_B_LAYERS, ATTN_DIM, D_MODEL), beta * ATTN_DIM ** -0.5)
    ffn_w_gate = nrm(ks[7], (DEPTH, D_MODEL, D_FF), D_MODEL ** -0.5)
    ffn_w_up = nrm(ks[8], (DEPTH, D_MODEL, D_FF), beta * D_MODEL ** -0.5)
    ffn_conv_w = nrm(ks[9], (DEPTH, CONV_WIDTH, D_FF), CONV_WIDTH ** -0.5)
    ffn_conv_b = nrm(ks[10], (DEPTH, D_FF), 0.02)
    ffn_w_down = nrm(ks[11], (DEPTH, D_FF, D_MODEL), beta * D_FF ** -0.5)
    ln1_g = 1.0 + nrm(ks[12], (DEPTH, D_MODEL), 0.05)
    ln1_b = nrm(ks[13], (DEPTH, D_MODEL), 0.02)
    ln2_g = 1.0 + nrm(ks[14], (DEPTH, D_MODEL), 0.05)
    ln2_b = nrm(ks[15], (DEPTH, D_MODEL), 0.02)
    return {"x": x, "pool_w": pool_w, "pool_scale": pool_scale, "w_q": w_q, "w_kv": w_kv,
            "w_o": w_o, "ffn_w_gate": ffn_w_gate, "ffn_w_up": ffn_w_up, "ffn_conv_w": ffn_conv_w,
            "ffn_conv_b": ffn_conv_b, "ffn_w_down": ffn_w_down, "ln1_g": ln1_g, "ln1_b": ln1_b,
            "ln2_g": ln2_g, "ln2_b": ln2_b}


def reference(x, pool_w, pool_scale, w_q, w_kv, w_o, ffn_w_gate, ffn_w_up, ffn_conv_w,
              ffn_conv_b, ffn_w_down, ln1_g, ln1_b, ln2_g, ln2_b):
    B, S, _ = x.shape
    cos, sin = rope_tables(S)
    k_shared = None
    v_shared = None
    for i in range(DEPTH):
        if i < N_A_LAYERS:
            mix = pool_mixer(x, pool_w[i], pool_scale[i])
        else:
            j = i - N_A_LAYERS
            mix = dilated_attention(x, k_shared, v_shared, w_q[j], w_o[j], cos, sin)
        x = layer_norm(DEEPNORM_ALPHA * x + mix, ln1_g[i], ln1_b[i])
        ffn = conv_ffn(x, ffn_w_gate[i], ffn_w_up[i], ffn_conv_w[i], ffn_conv_b[i], ffn_w_down[i])
        x = layer_norm(DEEPNORM_ALPHA * x + ffn, ln2_g[i], ln2_b[i])
        if i == N_A_LAYERS - 1:
            kv = (x @ w_kv).reshape(B, S, 2, N_GROUPS * N_HEADS, HEAD_DIM)
            k_shared = apply_rope(kv[:, :, 0], cos, sin).reshape(B, S, N_GROUPS, N_HEADS, HEAD_DIM)
            v_shared = kv[:, :, 1].reshape(B, S, N_GROUPS, N_HEADS, HEAD_DIM)
    return x
```

```python
import contextlib
import math
import numpy as np
import concourse.bass as bass
import concourse.mybir as mybir
from concourse.bass_utils import run_bass_kernel_spmd

F32 = mybir.dt.float32
BF16 = mybir.dt.bfloat16
AF = mybir.ActivationFunctionType
ALU = mybir.AluOpType

D = 1024
S = 2048
UT = 1024
NT = UT // 128
TC = 512
DFF = 2816
NJ = DFF // 128
DEPTH = 4
ALPHA = (2.0 * DEPTH) ** 0.25
EPS = 1e-5
POOL_W = (2, 4, 8, 16)
GROUPS = ((128, 1), (512, 4), (2048, 16))
SQC = math.sqrt(0.044715)
GELU_S = 2.0 * math.sqrt(2.0 / math.pi)
N_CORES = 8

ENGS = ("pe", "act", "dve", "pool", "sp")
SEM_ROT = 30000


class Prog:
    def __init__(self, nc, st):
        self.nc = nc
        self.st = st
        self.E = {"pe": nc.tensor, "act": nc.scalar, "dve": nc.vector, "pool": nc.gpsimd, "sp": nc.sync}
        self.nsem = 0
        self.esem = {}
        self.ecnt = {}
        for e in ENGS:
            self._new_esem(e)
        self.dsem = {}
        self.dcnt = {}
        self.waited = {e: {} for e in ENGS}
        self.last_w = {}
        self.readers = {}
        self.last_tok = {e: None for e in ENGS}
        self.dma_toks = {}
        self.pending = {e: [] for e in ENGS}
        self.ninstr = 0
        self.alias_toks = []
        self.seen = None

    def _new_sem(self, name):
        s = self.st.enter_context(self.nc.semaphore(name))
        self.nsem += 1
        return s

    def _new_esem(self, e):
        self.esem[e] = (self._new_sem("e_%s_%d" % (e, self.nsem)), self.nsem)
        self.ecnt[e] = 0

    def _wait(self, e, toks):
        w = self.waited[e]
        eng = self.E[e]
        for t in toks:
            sem, sid, cnt, _ = t
            if w.get(sid, 0) >= cnt:
                continue
            eng.wait_ge(sem, cnt)
            w[sid] = cnt
            self.ninstr += 1

    def _deps(self, eq, ecmp, reads, writes):
        toks = []
        lw = self.last_w
        rd = self.readers
        for k in reads:
            t = lw.get(k)
            if t is not None and not (ecmp == "pe" and t[3] == "pe"):
                toks.append(t)
        for k in writes:
            t = lw.get(k)
            if t is not None and (ecmp is None or t[3] != ecmp):
                toks.append(t)
            for t in rd.get(k, ()):
                if ecmp is None or t[3] != ecmp:
                    toks.append(t)
        if self.pending[eq]:
            toks.extend(self.pending[eq])
            self.pending[eq] = []
        if self.seen is not None:
            fresh = False
            for k in reads:
                if k not in self.seen:
                    self.seen.add(k)
                    fresh = fresh or not self._persistent(k)
            for k in writes:
                if k not in self.seen:
                    self.seen.add(k)
                    fresh = fresh or not self._persistent(k)
            if fresh:
                toks.extend(self.alias_toks)
        return toks

    def _record(self, tok, reads, writes):
        rd = self.readers
        for k in reads:
            l = rd.get(k)
            if l is None:
                rd[k] = [tok]
            else:
                l.append(tok)
        for k in writes:
            self.last_w[k] = tok
            rd[k] = []

    def op(self, e, fn, reads=(), writes=()):
        self._wait(e, self._deps(e, e, reads, writes))
        if self.ecnt[e] >= SEM_ROT:
            self._new_esem(e)
        ins = fn(self.E[e])
        sem, sid = self.esem[e]
        self.ecnt[e] += 1
        ins.then_inc(sem, 1)
        tok = (sem, sid, self.ecnt[e], e)
        self.last_tok[e] = tok
        self._record(tok, reads, writes)
        self.ninstr += 1
        return tok

    def dma(self, e, fn, key, n=1, reads=(), writes=()):
        self._wait(e, self._deps(e, None, reads, writes))
        if key not in self.dsem:
            self.dsem[key] = (self._new_sem("d_%s" % key), self.nsem)
            self.dcnt[key] = 0
        sem, sid = self.dsem[key]
        fn(self.E[e], sem)
        self.dcnt[key] += 16 * n
        tok = (sem, sid, self.dcnt[key], None)
        self.dma_toks[sid] = tok
        self._record(tok, reads, writes)
        self.ninstr += n
        return tok

    def barrier(self):
        toks = [t for t in self.last_tok.values() if t is not None] + list(self.dma_toks.values())
        for e in ENGS:
            self.pending[e] = list(toks)
        self.last_w = {}
        self.readers = {}
        self.dma_toks = {}
        self.alias_toks = []
        self.seen = None

    PERSIST = ("X", "halo", "cy", "ident", "poolB", "masks", "km", "eps", "lnp", "sta", "stb", "mv", "sd",
               "rs", "nm", "A0", "A1", "B0", "B1", "C0", "C1", "D0", "D1")

    @staticmethod
    def _persistent(k):
        if k in ("A0", "A1", "B0", "B1", "C0", "C1", "D0", "D1", "ident", "poolB", "masks", "km", "eps"):
            return True
        for pre in ("X", "halo", "cy", "lnp", "sta", "stb", "mv", "sd", "rs", "nm"):
            if k.startswith(pre) and (len(k) == len(pre) or k[len(pre)].isdigit() or k[len(pre)] in "bT"):
                return True
        return False

    def phase_end(self):
        best = {}
        for t in getattr(self, "alias_toks", []) or []:
            if t[1] not in best or best[t[1]][2] < t[2]:
                best[t[1]] = t
        dead = []
        for k, t in self.last_w.items():
            if not self._persistent(k):
                dead.append(k)
                if t[1] not in best or best[t[1]][2] < t[2]:
                    best[t[1]] = t
        for k, l in self.readers.items():
            if not self._persistent(k):
                if k not in self.last_w:
                    dead.append(k)
                for t in l:
                    if t[1] not in best or best[t[1]][2] < t[2]:
                        best[t[1]] = t
        for k in dead:
            self.last_w.pop(k, None)
            self.readers.pop(k, None)
        self.alias_toks = list(best.values())
        self.seen = set(self.last_w.keys()) | set(self.readers.keys())

    def finish(self):
        toks = [t for t in self.last_tok.values() if t is not None] + list(self.dma_toks.values())
        self._wait("sp", toks)


def build(nseq=4, stop=None, dump_kv=False):
    nc = bass.Bass("TRN2", target_bir_lowering=False)

    def din(name, shape):
        return nc.dram_tensor(name, list(shape), F32, kind="ExternalInput").ap()

    x_d = din("x", [nseq, S, D])
    pw_d = din("pw_l", [2, 128, 4, 2, 256])
    psc_d = din("pool_scale", [2, D])
    wq_d = din("wq_l", [2, 3, 8, 128, 8, 128])
    wk_d = din("wk_l", [3, 8, 128, 8, 128])
    wv_d = din("wv_l", [3, 128, 8, 1024])
    wo_d = din("wo_l", [2, 128, 8, 1024])
    wg_d = din("wg_l", [4, NJ, 128, 8, 128])
    wu_d = din("wu_l", [4, NJ, 128, 8, 128])
    wd_d = din("wd_l", [4, 128, NJ, 1024])
    cw_d = din("cw_l", [4, 128, NJ, 3])
    cb_d = din("cb_l", [4, 128, NJ])
    ln_d = [din(n, [4, D]) for n in ("ln1_g", "ln1_b", "ln2_g", "ln2_b")]
    ident_d = din("c_ident", [128, 128])
    poolB_d = din("c_poolB", [128, 12, 128])
    masks_d = din("c_masks", [128, 3, 512])
    km_d = din("c_km", [128, 2])
    cos_d = din("c_cos", [128, S])
    sin_d = din("c_sin", [128, S])
    out_d = nc.dram_tensor("out", [nseq, S, D], F32, kind="ExternalOutput").ap()
    kvkind = "ExternalOutput" if dump_kv else "Internal"
    kt_hbm = nc.dram_tensor("kt_hbm", [3, 8, 2, 128, S], BF16, kind=kvkind).ap()
    v_hbm = nc.dram_tensor("v_hbm", [S, 3, 16, 128], BF16, kind=kvkind).ap()

    uid = [0]

    with contextlib.ExitStack() as st:
        p = Prog(nc, st)

        def sbuf(stack, name, shape, dt=F32):
            uid[0] += 1
            return stack.enter_context(nc.sbuf_tensor("%s_%d" % (name, uid[0]), list(shape), dt))

        X = sbuf(st, "X", [128, NT, D])
        Xb = sbuf(st, "Xb", [128, NT, D], BF16)
        XT = sbuf(st, "XT", [128, 8, UT], BF16)
        halo = sbuf(st, "halo", [128, 2, D], BF16)
        carry = sbuf(st, "carry", [128, 4, NJ, 2])
        ident = sbuf(st, "ident", [128, 128], BF16)
        poolB = sbuf(st, "poolB", [128, 12, 128], BF16)
        masks = sbuf(st, "masks", [128, 3, 512], BF16)
        km = sbuf(st, "km", [128, 2])
        epsT = sbuf(st, "epsT", [128, 1])
        lnp = sbuf(st, "lnp", [128, 4, D])
        stats = sbuf(st, "stats", [128, 4, 2, 6])
        mv = sbuf(st, "mv", [128, 4, 2])
        sd = sbuf(st, "sd", [128, 4, 4])
        PSA = st.enter_context(nc.psum_tensor("PSA", [128, 1024], F32))
        PSB = st.enter_context(nc.psum_tensor("PSB", [128, 1024], F32))
        PSD = st.enter_context(nc.psum_tensor("PSD", [128, 1024], F32))
        PSC = [st.enter_context(nc.psum_tensor("PSC%d" % i, [128, 512], F32)) for i in range(2)]
        TRB = [PSC[i][:, :].bitcast(BF16).rearrange("p (k n) -> p k n", n=128) for i in range(2)]
        kA = ["A0", "A1"]
        kB = ["B0", "B1"]
        kD = ["D0", "D1"]
        kC = ["C0", "C1"]

        def ld_consts(e, s):
            e.dma_start(out=ident[:], in_=ident_d[:, :]).then_inc(s, 16)
            e.dma_start(out=poolB[:], in_=poolB_d[:, :, :]).then_inc(s, 16)
            e.dma_start(out=masks[:], in_=masks_d[:, :, :]).then_inc(s, 16)
        p.dma("pool", ld_consts, "const", n=3, writes=["ident", "poolB", "masks"])
        p.dma("sp", lambda e, s: e.dma_start(out=km[:], in_=km_d[:, :]).then_inc(s, 16), "const2", writes=["km"])
        p.op("dve", lambda e: e.memset(epsT[:], EPS), writes=["eps"])

        par_state = {"ln": 0, "xt": 0}

        ln_ctr = [0]

        class LNPipe:
            NST = 5

            def __init__(self, gi, bi, want_xt):
                self.items = []
                self.gi, self.bi, self.want_xt = gi, bi, want_xt

            def _stage(self, k, item):
                i, ps_ap, pskeys, gidx = item
                par = gidx % 4
                gi, bi = self.gi, self.bi
                Xi = X[:, i, :]
                kx = "X%d" % i
                sa, sb_, smv, ssd, srs, snm = ("sta%d" % par, "stb%d" % par, "mv%d" % par, "sd%d" % par,
                                               "rs%d" % par, "nm%d" % par)
                if k == 0:
                    p.op("dve", lambda e: e.scalar_tensor_tensor(Xi, Xi, ALPHA, ps_ap, ALU.mult, ALU.add),
                         reads=list(pskeys) + [kx], writes=[kx])
                    p.op("dve", lambda e: e.bn_stats(stats[:, par, 0, :], X[:, i, 0:512]), reads=[kx], writes=[sa])
                    p.op("dve", lambda e: e.bn_stats(stats[:, par, 1, :], X[:, i, 512:1024]), reads=[kx], writes=[sb_])
                    p.op("dve", lambda e: e.bn_aggr(mv[:, par, :], stats[:, par, :, :]), reads=[sa, sb_], writes=[smv])
                elif k == 1:
                    p.op("act", lambda e: e.activation(sd[:, par, 0:1], mv[:, par, 1:2], AF.Sqrt, bias=epsT[:], scale=1.0),
                         reads=[smv, "eps"], writes=[ssd])
                    p.op("dve", lambda e: e.reciprocal(sd[:, par, 1:2], sd[:, par, 0:1]), reads=[ssd], writes=[srs])
                    p.op("dve", lambda e: e.tensor_scalar(sd[:, par, 2:3], mv[:, par, 0:1], sd[:, par, 1:2], -1.0,
                                                          ALU.mult, ALU.mult), reads=[smv, srs], writes=[snm])
                elif k == 2:
                    p.op("act", lambda e: e.activation(Xi, Xi, AF.Identity, bias=sd[:, par, 2:3], scale=sd[:, par, 1:2]),
                         reads=[kx, srs, snm], writes=[kx])
                    p.op("dve", lambda e: e.tensor_tensor(Xi, Xi, lnp[:, gi, :], ALU.mult), reads=[kx, "lnp%d" % gi], writes=[kx])
                    p.op("dve", lambda e: e.tensor_tensor(Xi, Xi, lnp[:, bi, :], ALU.add), reads=[kx, "lnp%d" % bi], writes=[kx])
                elif k == 3:
                    p.op("act", lambda e: e.activation(Xb[:, i, :], Xi, AF.Identity), reads=[kx], writes=["Xb%d" % i])
                elif k == 4 and self.want_xt:
                    xt_tile(i)

            def _step(self, s):
                n = len(self.items)
                for k in range(self.NST):
                    j = s - k
                    if 0 <= j < n:
                        self._stage(k, self.items[j])

            def push(self, i, ps_ap, pskeys):
                self.items.append((i, ps_ap, pskeys, ln_ctr[0]))
                ln_ctr[0] += 1
                self._step(len(self.items) - 1)

            def flush(self):
                n = len(self.items)
                for s in range(n, n + self.NST - 1):
                    self._step(s)
                self.items = []

        def xt_tile(i):
            par = par_state["xt"]
            par_state["xt"] ^= 1
            trp = TRB[par]

            def f(e):
                for k in range(8):
                    ins = e.transpose(trp[:, k, :], Xb[:, i, k * 128:(k + 1) * 128], ident[:])
                return ins
            p.op("pe", f, reads=["Xb%d" % i, "ident"], writes=[kC[par]])
            p.op("act", lambda e: e.activation(XT[:, :, i * 128:(i + 1) * 128], trp, AF.Identity),
                 reads=[kC[par]], writes=["XT%d" % i])

        def load_ln(l):
            def f(e, s):
                for q in range(4):
                    e.dma_start(out=lnp[:, q, :], in_=ln_d[q][l, :].partition_broadcast(128)).then_inc(s, 16)
            p.dma("sp", f, "lnp", n=4, writes=["lnp0", "lnp1", "lnp2", "lnp3"])

        def load_unit(s, c):
            for i in range(NT):
                r0 = c * UT + i * 128
                p.dma("sp", (lambda i, r0: lambda e, sm: e.dma_start(out=X[:, i, :], in_=x_d[s, r0:r0 + 128, :]).then_inc(sm, 16))(i, r0),
                      "ldx%d" % i, writes=["X%d" % i])
            for i in range(NT):
                if i % 2 == 0:
                    p.op("act", (lambda i: lambda e: e.activation(Xb[:, i, :], X[:, i, :], AF.Identity))(i),
                         reads=["X%d" % i], writes=["Xb%d" % i])
                else:
                    p.op("dve", (lambda i: lambda e: e.tensor_copy(Xb[:, i, :], X[:, i, :]))(i),
                         reads=["X%d" % i], writes=["Xb%d" % i])
            if c == 0:
                p.op("pool", lambda e: e.memset(carry[:], 0.0),
                     writes=["cy%d_%d" % (l, j) for l in range(4) for j in range(NJ)])

        def store_unit(s, c):
            for i in range(NT):
                r0 = c * UT + i * 128
                p.dma("sp", (lambda i, r0: lambda e, sm: e.dma_start(out=out_d[s, r0:r0 + 128, :], in_=X[:, i, :]).then_inc(sm, 16))(i, r0),
                      "stx%d" % i, reads=["X%d" % i])

        def pool_phase(l, c, want_xt):
            with contextlib.ExitStack() as ph:
                pwf = sbuf(ph, "pwf", [128, 4, 2, 256], BF16)
                pws = sbuf(ph, "pws", [128, 4, 2, 256], BF16)
                psc = sbuf(ph, "psc", [128, D])
                pmT = sbuf(ph, "pmT", [128, 2, 8, 128], BF16)
                p.dma("pool", lambda e, s: e.dma_start(out=pwf[:], in_=pw_d[l]).then_inc(s, 16), "pwf", writes=["pwf"])
                p.dma("sp", lambda e, s: e.dma_start(out=psc[:], in_=psc_d[l, :].partition_broadcast(128)).then_inc(s, 16),
                      "psc", writes=["psc"])
                for g in range(4):
                    for kc in range(2):
                        p.op("pool", (lambda g, kc: lambda e: e.tensor_tensor(
                            pws[:, g, kc, :], pwf[:, g, kc, :], psc[:, g * 256:(g + 1) * 256], ALU.mult))(g, kc),
                            reads=["pwf", "psc"], writes=["pws%d%d" % (g, kc)])
                if c == 0:
                    p.op("act", lambda e: e.activation(halo[:, l, :], Xb[:, NT - 1, :], AF.Identity),
                         reads=["Xb%d" % (NT - 1)], writes=["halo%d" % l])
                pwskeys = ["pws%d%d" % (g, kc) for g in range(4) for kc in range(2)]

                def pm(i):
                    ps = PSA if i % 2 == 0 else PSB
                    keys = kA if i % 2 == 0 else kB
                    first = (c == 0 and i == 0)
                    hasprev = not first
                    rk = ["Xb%d" % i, "poolB"]
                    if hasprev:
                        rk.append("Xb%d" % (i - 1) if i > 0 else "halo%d" % l)

                    def f(e):
                        for cc in range(8):
                            g = cc // 2
                            bsel = poolB[:, g * 3 + (2 if first else 0), :]
                            ins = e.matmul(ps[:, cc * 128:(cc + 1) * 128], lhsT=Xb[:, i, cc * 128:(cc + 1) * 128],
                                           rhs=bsel, start=True, stop=not hasprev)
                            if hasprev:
                                src = Xb[:, i - 1, cc * 128:(cc + 1) * 128] if i > 0 else halo[:, l, cc * 128:(cc + 1) * 128]
                                ins = e.matmul(ps[:, cc * 128:(cc + 1) * 128], lhsT=src, rhs=poolB[:, g * 3 + 1, :],
                                               start=False, stop=True)
                        return ins
                    p.op("pe", f, reads=rk, writes=keys)
                    p.op("act", lambda e: e.activation(pmT[:, i % 2, :, :], ps[:, :].rearrange("p (k n) -> p k n", n=128),
                                                       AF.Identity), reads=keys, writes=["pmT%d" % (i % 2)])

                def rest(i):
                    def f(e):
                        for g in range(4):
                            for kc in range(2):
                                ins = e.matmul(PSD[:, g * 256:(g + 1) * 256], lhsT=pmT[:, i % 2, 2 * g + kc, :],
                                               rhs=pws[:, g, kc, :], start=(kc == 0), stop=(kc == 1))
                        return ins
                    p.op("pe", f, reads=["pmT%d" % (i % 2)] + pwskeys, writes=kD)

                pipe = LNPipe(0, 1, want_xt)
                pm(0)
                for i in range(NT):
                    if i + 1 < NT:
                        pm(i + 1)
                    rest(i)
                    pipe.push(i, PSD[:, :], kD)
                pipe.flush()
                p.phase_end()

        def ffn_phase(l, c, want_xt):
            with contextlib.ExitStack() as ph:
                hT = sbuf(ph, "hT", [128, NJ, TC], BF16)
                Wd = sbuf(ph, "Wd", [128, NJ, D], BF16)
                wg = [sbuf(ph, "wg%d" % i, [128, 8, 128], BF16) for i in range(4)]
                wu = [sbuf(ph, "wu%d" % i, [128, 8, 128], BF16) for i in range(4)]
                gbuf = sbuf(ph, "gbuf", [128, 2, TC + 2])
                cbf = sbuf(ph, "cbf", [128, 3, TC])
                sq = sbuf(ph, "sq", [128, 2, TC])
                sg = sbuf(ph, "sg", [128, 2, TC])
                cu = sbuf(ph, "cu", [128, 2, TC])
                cw = sbuf(ph, "cw", [128, NJ, 3])
                cbias = sbuf(ph, "cbias", [128, NJ])

                def ldc(e, s):
                    e.dma_start(out=cw[:], in_=cw_d[l]).then_inc(s, 16)
                    e.dma_start(out=cbias[:], in_=cb_d[l]).then_inc(s, 16)
                p.dma("sp", ldc, "convp", n=2, writes=["cw"])

                def ld_w(q):
                    slot = q % 4
                    j = q % NJ

                    def f(e, s):
                        e.dma_start(out=wg[slot][:], in_=wg_d[l, j]).then_inc(s, 16)
                        e.dma_start(out=wu[slot][:], in_=wu_d[l, j]).then_inc(s, 16)
                    p.dma("pool", f, "wgu%d" % slot, n=2, writes=["wgu%d" % slot])

                ld_w(0)
                ld_w(1)
                bounds = [0, 6, 12, 17, NJ]

                def ldwd(e, s):
                    for q in range(4):
                        a, b = bounds[q], bounds[q + 1]
                        e.dma_start(out=Wd[:, a:b, :], in_=wd_d[l][:, a:b, :]).then_inc(s, 16)
                NQ = 2 * NJ

                def st_pe(q):
                    ch, j = divmod(q, NJ)
                    par = q % 2
                    slot = q % 4
                    gp = PSA[:, par * 512:(par + 1) * 512]
                    up = PSB[:, par * 512:(par + 1) * 512]
                    xtk = ["XT%d" % (ch * 4 + t) for t in range(4)]

                    def fg(e):
                        for k in range(8):
                            ins = e.matmul(gp, lhsT=wg[slot][:, k, :], rhs=XT[:, k, ch * TC:(ch + 1) * TC],
                                           start=(k == 0), stop=(k == 7))
                        return ins

                    def fu(e):
                        for k in range(8):
                            ins = e.matmul(up, lhsT=wu[slot][:, k, :], rhs=XT[:, k, ch * TC:(ch + 1) * TC],
                                           start=(k == 0), stop=(k == 7))
                        return ins
                    p.op("pe", fg, reads=["wgu%d" % slot] + xtk, writes=[kA[par]])
                    p.op("pe", fu, reads=["wgu%d" % slot] + xtk, writes=[kB[par]])

                def st_s1(q):
                    ch, j = divmod(q, NJ)
                    par = q % 2
                    c3 = q % 3
                    gp = PSA[:, par * 512:(par + 1) * 512]
                    gk = kA[par]
                    cbp = cbf[:, c3, :]
                    kcy = "cy%d_%d" % (l, j)
                    kgh, kgb, kc_ = "gbh%d" % par, "gbb%d" % par, "cb%d" % c3
                    p.op("pool", lambda e: e.tensor_copy(gbuf[:, par, 0:2], carry[:, l, j, :]), reads=[kcy], writes=[kgh])
                    p.op("act", lambda e: e.activation(gbuf[:, par, 2:TC + 2], gp, AF.Identity), reads=[gk], writes=[kgb])
                    p.op("act", lambda e: e.activation(cbp, gp, AF.Identity, bias=cbias[:, j:j + 1], scale=cw[:, j, 2:3]),
                         reads=[gk, "cw"], writes=[kc_])
                    p.op("pool", lambda e: e.tensor_copy(carry[:, l, j, :], gbuf[:, par, TC:TC + 2]), reads=[kgb], writes=[kcy])
                    p.op("dve", lambda e: e.scalar_tensor_tensor(cbp, gbuf[:, par, 1:TC + 1], cw[:, j, 1:2], cbp, ALU.mult, ALU.add),
                         reads=[kgb, kgh, kc_, "cw"], writes=[kc_])
                    p.op("dve", lambda e: e.scalar_tensor_tensor(cbp, gbuf[:, par, 0:TC], cw[:, j, 0:1], cbp, ALU.mult, ALU.add),
                         reads=[kgb, kgh, kc_, "cw"], writes=[kc_])

                def st_s2(q):
                    par = q % 2
                    c3 = q % 3
                    cbp = cbf[:, c3, :]
                    sqp = sq[:, par, :]
                    kc_, ksq = "cb%d" % c3, "sq%d" % par
                    p.op("act", lambda e: e.activation(sqp, cbp, AF.Square, scale=SQC), reads=[kc_], writes=[ksq])
                    p.op("dve", lambda e: e.scalar_tensor_tensor(sqp, sqp, 1.0, cbp, ALU.add, ALU.mult), reads=[ksq, kc_], writes=[ksq])

                def st_d4(q):
                    par = q % 2
                    c3 = q % 3
                    up = PSB[:, par * 512:(par + 1) * 512]
                    p.op("dve", lambda e: e.tensor_tensor(cu[:, par, :], cbf[:, c3, :], up, ALU.mult),
                         reads=["cb%d" % c3, kB[par]], writes=["cu%d" % par])

                def st_s3(q):
                    ch, j = divmod(q, NJ)
                    par = q % 2
                    p.op("act", lambda e: e.activation(sg[:, par, :], sq[:, par, :], AF.Sigmoid, scale=GELU_S),
                         reads=["sq%d" % par], writes=["sg%d" % par])
                    p.op("dve", lambda e: e.tensor_tensor(hT[:, j, :], cu[:, par, :], sg[:, par, :], ALU.mult),
                         reads=["cu%d" % par, "sg%d" % par], writes=["hT%d" % j])

                def step(t):
                    if 0 <= t - 2 < NQ:
                        st_d4(t - 2)
                    if t < NQ:
                        if t + 2 < NQ:
                            ld_w(t + 2)
                        if t == 3:
                            p.dma("pool", ldwd, "Wd", n=4, writes=["Wd"])
                        st_pe(t)
                        st_s1(t)
                    if 0 <= t - 1 < NQ:
                        st_s2(t - 1)
                    if 0 <= t - 2 < NQ:
                        st_s3(t - 2)

                def down(ch, t):
                    i = ch * 4 + t

                    def f(e):
                        for half in range(2):
                            for j in range(NJ):
                                ins = e.matmul(PSD[:, half * 512:(half + 1) * 512], lhsT=hT[:, j, t * 128:(t + 1) * 128],
                                               rhs=Wd[:, j, half * 512:(half + 1) * 512], start=(j == 0), stop=(j == NJ - 1))
                        return ins
                    p.op("pe", f, reads=["hT%d" % j for j in range(NJ)] + ["Wd"], writes=kD)

                def downs(ch):
                    pipe = LNPipe(2, 3, want_xt)
                    for t in range(4):
                        down(ch, t)
                        pipe.push(ch * 4 + t, PSD[:, :], kD)
                    pipe.flush()

                for t in range(NQ + 2):
                    step(t)
                    if t == NJ + 1:
                        downs(0)
                downs(1)
                p.phase_end()

        def rope(ps_ap, pskey, cosap, sinap, ra, ru, par, out_fn):
            kra, kru0, kru1 = "ra%d" % par, "ru0_%d" % par, "ru1_%d" % par
            p.op("dve", lambda e: e.tensor_tensor(ra[:, par, :], ps_ap, cosap, ALU.mult), reads=[pskey, "rope"], writes=[kra])
            p.op("dve", lambda e: e.tensor_tensor(ru[64:128, par, :], ps_ap[0:64, :], sinap[0:64, :], ALU.mult),
                 reads=[pskey, "rope"], writes=[kru1])
            p.op("dve", lambda e: e.tensor_tensor(ru[0:64, par, :], ps_ap[64:128, :], sinap[64:128, :], ALU.mult),
                 reads=[pskey, "rope"], writes=[kru0])
            out_fn(ra[:, par, :], ru[:, par, :], [kra, kru0, kru1])

        def kv_phase(c):
            with contextlib.ExitStack() as ph:
                cosU = sbuf(ph, "cosU", [128, UT])
                sinU = sbuf(ph, "sinU", [128, UT])
                wvb = [sbuf(ph, "wv%d" % i, [128, 8, D], BF16) for i in range(2)]
                wk = [sbuf(ph, "wk%d" % i, [128, 8, 128], BF16) for i in range(4)]
                kst = [sbuf(ph, "kst%d" % i, [128, 2, UT], BF16) for i in range(2)]
                vst = [sbuf(ph, "vst%d" % i, [128, 16, 128], BF16) for i in range(2)]
                ra = sbuf(ph, "ra", [128, 2, TC])
                ru = sbuf(ph, "ru", [128, 2, TC])
                rs = sbuf(ph, "rs", [128, 2, TC])

                def ldr(e, s):
                    e.dma_start(out=cosU[:], in_=cos_d[:, c * UT:(c + 1) * UT]).then_inc(s, 16)
                    e.dma_start(out=sinU[:], in_=sin_d[:, c * UT:(c + 1) * UT]).then_inc(s, 16)
                p.dma("sp", ldr, "rope", n=2, writes=["rope"])
                for i in range(2):
                    p.op("pool", (lambda i: lambda e: e.memset(vst[i][:], 1.0))(i), writes=["vst%d" % i, "vst%db" % i])
                xtall = ["XT%d" % i for i in range(NT)]
                cnt = 0
                vcnt = 0
                def ld_wk(n):
                    g_, hp_ = divmod(n, 8)
                    sl_ = n % 4
                    p.dma("pool", lambda e, s: e.dma_start(out=wk[sl_][:], in_=wk_d[g_, hp_]).then_inc(s, 16),
                          "wk%d" % sl_, writes=["wk%d" % sl_])

                def ld_wv(g_):
                    p.dma("pool", lambda e, s: e.dma_start(out=wvb[g_ % 2][:], in_=wv_d[g_]).then_inc(s, 16),
                          "wv%d" % (g_ % 2), writes=["wv%d" % (g_ % 2)])
                ld_wk(0)
                ld_wk(1)
                ld_wv(0)
                for g in range(3):
                    wv = wvb[g % 2]
                    wvk = "wv%d" % (g % 2)
                    if g + 1 < 3:
                        ld_wv(g + 1)
                    for hp in range(8):
                        sl = cnt % 4
                        ks = cnt % 2
                        if cnt + 2 < 24:
                            ld_wk(cnt + 2)
                        cnt += 1
                        for tc in range(2):
                            par = tc
                            kp = PSA[:, par * 512:(par + 1) * 512]

                            def fk(e, sl=sl, tc=tc, kp=kp):
                                for k in range(8):
                                    ins = e.matmul(kp, lhsT=wk[sl][:, k, :], rhs=XT[:, k, tc * TC:(tc + 1) * TC],
                                                   start=(k == 0), stop=(k == 7))
                                return ins
                            p.op("pe", fk, reads=["wk%d" % sl] + xtall[tc * 4:(tc + 1) * 4], writes=[kA[par]])

                            def fin(ra_ap, ru_ap, keys, sl=ks, tc=tc, par=par):
                                p.op("dve", lambda e: e.tensor_tensor(rs[:, par, :], ra_ap, ru_ap, ALU.add), reads=keys, writes=["krs%d" % par])
                                for h in range(2):
                                    p.op("act", (lambda h: lambda e: e.activation(kst[sl][:, h, tc * TC:(tc + 1) * TC], rs[:, par, :],
                                                                                  AF.Identity, scale=km[:, h:h + 1]))(h),
                                         reads=["krs%d" % par, "km"], writes=["kst%d_%d_%d" % (sl, h, tc)])
                            rope(kp, kA[par], cosU[:, tc * TC:(tc + 1) * TC], sinU[:, tc * TC:(tc + 1) * TC], ra, ru, par, fin)
                        dst = kt_hbm[g, hp, :, :, c * UT:(c + 1) * UT].rearrange("h q t -> q h t")
                        p.dma("sp", (lambda dst, sl: lambda e, s: e.dma_start(out=dst, in_=kst[sl][:]).then_inc(s, 16))(dst, ks),
                              "kst%d" % ks, reads=["kst%d_%d_%d" % (ks, h, tc) for h in range(2) for tc in range(2)])
                    for t in range(NT):
                        vs = vcnt % 2
                        vcnt += 1
                        vp = PSB if vs == 0 else PSD
                        vk = kB if vs == 0 else kD

                        def fv(e, t=t, vp=vp, wv=wv):
                            for half in range(2):
                                for k in range(8):
                                    ins = e.matmul(vp[:, half * 512:(half + 1) * 512], lhsT=XT[:, k, t * 128:(t + 1) * 128],
                                                   rhs=wv[:, k, half * 512:(half + 1) * 512], start=(k == 0), stop=(k == 7))
                            return ins
                        p.op("pe", fv, reads=[wvk, "XT%d" % t], writes=vk)
                        vview = vp[:, :].rearrange("p (h c) -> p h c", c=64)
                        p.op("act", (lambda vs, vview: lambda e: e.activation(vst[vs][:, 0:16:2, 0:64], vview[:, 0:16:2, :], AF.Identity))(vs, vview),
                             reads=vk, writes=["vst%d" % vs])
                        p.op("act", (lambda vs, vview: lambda e: e.activation(vst[vs][:, 1:16:2, 64:128], vview[:, 1:16:2, :], AF.Identity))(vs, vview),
                             reads=vk, writes=["vst%db" % vs])
                        r0 = c * UT + t * 128
                        p.dma("sp", (lambda vs, g, r0: lambda e, s: e.dma_start(out=v_hbm[r0:r0 + 128, g, :, :], in_=vst[vs][:]).then_inc(s, 16))(vs, g, r0),
                              "vst%d" % vs, reads=["vst%d" % vs, "vst%db" % vs])
                p.phase_end()

        def attn_phase(jl, c):
            KL = UT * (c + 1)
            with contextlib.ExitStack() as ph:
                cosU = sbuf(ph, "cosU", [128, UT])
                sinU = sbuf(ph, "sinU", [128, UT])
                OT = sbuf(ph, "OT", [128, 8, UT], BF16)
                Wo = sbuf(ph, "Wo", [128, 8, D], BF16)
                wq = [sbuf(ph, "wq%d" % i, [128, 8, 128], BF16) for i in range(3)]
                QT = [sbuf(ph, "QT%d" % i, [128, UT], BF16) for i in range(2)]
                kt = [sbuf(ph, "kt%d" % i, [128, 2, S], BF16) for i in range(2)]
                vb = [sbuf(ph, "vb%d" % i, [128, 16, 256], BF16) for i in range(2)]
                esb = [sbuf(ph, "esb%d" % i, [128, 512], BF16) for i in range(2)]
                PT = [sbuf(ph, "PT%d" % i, [128, 512], BF16) for i in range(3)]
                rden = sbuf(ph, "rden", [128, UT])
                Osb = sbuf(ph, "Osb", [128, 2, UT])
                ra = sbuf(ph, "ra", [128, 2, TC])
                ru = sbuf(ph, "ru", [128, 2, TC])

                def ldr(e, s):
                    e.dma_start(out=cosU[:], in_=cos_d[:, c * UT:(c + 1) * UT]).then_inc(s, 16)
                    e.dma_start(out=sinU[:], in_=sin_d[:, c * UT:(c + 1) * UT]).then_inc(s, 16)
                p.dma("sp", ldr, "rope", n=2, writes=["rope"])
                p.dma("pool", lambda e, s: e.dma_start(out=Wo[:], in_=wo_d[jl]).then_inc(s, 16), "Wo", writes=["Wo"])
                xtall = ["XT%d" % i for i in range(NT)]
                its = [(hp, g) for hp in range(8) for g in range(3)]

                def ld_wq(it):
                    hp, g = its[it]
                    s3 = it % 3
                    p.dma("pool", lambda e, s: e.dma_start(out=wq[s3][:], in_=wq_d[jl, g, hp]).then_inc(s, 16), "wq%d" % s3, writes=["wq%d" % s3])

                def loads(it):
                    hp, g = its[it]
                    sl = it % 2
                    d = GROUPS[g][1]
                    NB = 16 // d
                    src = kt_hbm[g, hp, :, :, 0:KL].rearrange("h q t -> q h t")
                    p.dma("sp", lambda e, s: e.dma_start(out=kt[sl][:, :, 0:KL], in_=src).then_inc(s, 16), "kt%d" % sl, writes=["kt%d" % sl])
                    vsrc = v_hbm[:, g, 2 * hp:2 * hp + 2, :].rearrange("(n i r) h c -> i r n (h c)", n=NB, i=128, r=d)
                    vdst = vb[sl][:, :, :].rearrange("p (r n) x -> p r n x", r=d, n=NB)

                    def fv(e, s):
                        for r in range(d):
                            if d == 16:
                                rows = 64 * (c + 1)
                                e.dma_start(out=vdst[0:rows, r, 0, :], in_=vsrc[0:rows, r, 0, :]).then_inc(s, 16)
                            else:
                                nbv = NB // 2 * (c + 1)
                                e.dma_start(out=vdst[:, r, 0:nbv, :], in_=vsrc[:, r, 0:nbv, :]).then_inc(s, 16)
                    p.dma("sp", fv, "vb%d" % sl, n=d, writes=["vb%d" % sl])

                def qproj(it):
                    hp, g = its[it]
                    sl = it % 2
                    for tc in range(2):
                        par = tc
                        qp = PSA[:, par * 512:(par + 1) * 512]

                        def fq(e, tc=tc, qp=qp):
                            for k in range(8):
                                ins = e.matmul(qp, lhsT=wq[it % 3][:, k, :], rhs=XT[:, k, tc * TC:(tc + 1) * TC],
                                               start=(k == 0), stop=(k == 7))
                            return ins
                        p.op("pe", fq, reads=["wq%d" % (it % 3)] + xtall[tc * 4:(tc + 1) * 4], writes=[kA[par]])

                        def fin(ra_ap, ru_ap, keys, tc=tc):
                            p.op("dve", lambda e: e.tensor_tensor(QT[sl][:, tc * TC:(tc + 1) * TC], ra_ap, ru_ap, ALU.add),
                                 reads=keys, writes=["QT%d_%d" % (sl, tc)])
                        rope(qp, kA[par], cosU[:, tc * TC:(tc + 1) * TC], sinU[:, tc * TC:(tc + 1) * TC], ra, ru, par, fin)

                batches = []
                for it, (hp, g) in enumerate(its):
                    d = GROUPS[g][1]
                    NB = 16 // d
                    for h in range(2):
                        items = []
                        if d == 16:
                            kc = 64 * (c + 1)
                            for r in range(16):
                                items.append(dict(kc=kc, kstart=r, blk=r * NB, parts=[(0, r, 64)]))
                            groups_ = [items[0:8], items[8:16]]
                            mt = 1 + c
                            slotw = 64
                        else:
                            nblk = 1024 // (128 * d)
                            qlo, qhi = nblk * c, nblk * (c + 1)
                            for r in range(d):
                                for n in range(max(qlo - 1, 0), qhi):
                                    has_cur = qlo <= n < qhi
                                    has_prev = qlo <= n + 1 < qhi
                                    qc = 128 * n * d + r - c * UT
                                    qn = 128 * (n + 1) * d + r - c * UT
                                    if has_cur and has_prev:
                                        parts = [(0, qc, 256)]
                                    elif has_cur:
                                        parts = [(0, qc, 128)]
                                    else:
                                        parts = [(128, qn, 128)]
                                    items.append(dict(kc=128, kstart=128 * n * d + r, blk=r * NB + n, parts=parts))
                            groups_ = [items[i:i + 2] for i in range(0, len(items), 2)]
                            mt = 0
                            slotw = 256
                        for grp in groups_:
                            batches.append(dict(it=it, hp=hp, g=g, h=h, d=d, items=grp, mt=mt, slotw=slotw))

                def qk(bi):
                    b = batches[bi]
                    sl = b["it"] % 2
                    bp = bi % 2
                    Sp = PSC[bp]
                    d, h = b["d"], b["h"]

                    def f(e):
                        for ci, itx in enumerate(b["items"]):
                            kc = itx["kc"]
                            ks = itx["kstart"]
                            base = ci * b["slotw"]
                            for (so, q0, nq) in itx["parts"]:
                                ins = e.matmul(Sp[0:kc, base + so:base + so + nq],
                                               lhsT=kt[sl][:, h, ks:ks + (kc - 1) * d + 1:d],
                                               rhs=QT[sl][:, q0:q0 + (nq - 1) * d + 1:d], start=True, stop=True)
                        return ins
                    p.op("pe", f, reads=["kt%d" % sl, "QT%d_0" % sl, "QT%d_1" % sl], writes=[kC[bp]])
                    rows = b["items"][0]["kc"]
                    p.op("act", lambda e: e.activation(esb[bp][0:rows, :], Sp[0:rows, :], AF.Exp, scale=0.125),
                         reads=[kC[bp]], writes=["esb%d" % bp])
                    p3 = bi % 3
                    p.op("dve", lambda e: e.tensor_tensor(PT[p3][0:rows, :], esb[bp][0:rows, :], masks[0:rows, b["mt"], :], ALU.mult),
                         reads=["esb%d" % bp, "masks"], writes=["PT%d" % p3])

                def pv(bi, last):
                    b = batches[bi]
                    sl = b["it"] % 2
                    bp = bi % 3
                    d, h = b["d"], b["h"]
                    Op = PSB if h == 0 else PSD
                    ok = kB if h == 0 else kD

                    def f(e):
                        nmm = sum(len(itx["parts"]) for itx in b["items"])
                        m = 0
                        for ci, itx in enumerate(b["items"]):
                            kc = itx["kc"]
                            base = ci * b["slotw"]
                            for (so, q0, nq) in itx["parts"]:
                                m += 1
                                ins = e.matmul(Op[:, q0:q0 + (nq - 1) * d + 1:d],
                                               lhsT=vb[sl][0:kc, itx["blk"], h * 128:(h + 1) * 128],
                                               rhs=PT[bp][0:kc, base + so:base + so + nq],
                                               start=False, stop=(last and m == nmm), skip_group_check=True)
                        return ins
                    p.op("pe", f, reads=["vb%d" % sl, "PT%d" % bp], writes=ok)

                deferred = []

                def evac_head(hp, h, more):
                    Op = PSB if h == 0 else PSD
                    ok = kB if h == 0 else kD
                    p.op("act", lambda e: e.activation(Osb[:, h, :], Op[:, :], AF.Identity), reads=ok, writes=["Osb%d" % h])
                    if more:
                        p.op("dve", lambda e: e.memset(Op[:, :], 0.0), writes=ok)
                    if h == 0:
                        nsl, dsl = slice(0, 64), slice(64, 128)
                    else:
                        nsl, dsl = slice(64, 128), slice(0, 64)
                    kr = "rden%d" % h
                    deferred.append(lambda: p.op("act", lambda e: e.activation(rden[nsl, :], Osb[dsl, h, :], AF.Ln),
                                                 reads=["Osb%d" % h], writes=[kr]))
                    deferred.append(lambda: p.op("act", lambda e: e.activation(rden[nsl, :], rden[nsl, :], AF.Exp, scale=-1.0),
                                                 reads=[kr], writes=[kr]))
                    deferred.append(lambda: p.op("dve", lambda e: e.tensor_tensor(OT[nsl, hp, :], Osb[nsl, h, :], rden[nsl, :], ALU.mult),
                                                 reads=["Osb%d" % h, kr], writes=["OT%d_%d" % (hp, h)]))

                ld_wq(0)
                ld_wq(1)
                ld_wq(2)
                loads(0)
                qproj(0)
                loads(1)
                qproj(1)
                p.op("dve", lambda e: e.memset(PSB[:, :], 0.0), writes=kB)
                p.op("dve", lambda e: e.memset(PSD[:, :], 0.0), writes=kD)
                nbt = len(batches)
                last_of = {}
                last_it = {}
                for bi, b in enumerate(batches):
                    last_of[(b["hp"], b["h"])] = bi
                    last_it[b["it"]] = bi
                SKEW = 2
                for bi in range(nbt + SKEW):
                    if bi < nbt:
                        qk(bi)
                    if deferred:
                        deferred.pop(0)()
                    k = bi - SKEW
                    if k >= 0:
                        pb = batches[k]
                        is_last = last_of[(pb["hp"], pb["h"])] == k
                        pv(k, is_last)
                        if is_last:
                            evac_head(pb["hp"], pb["h"], pb["hp"] < 7)
                        if last_it[pb["it"]] == k and pb["it"] + 2 < len(its):
                            if pb["it"] + 3 < len(its):
                                ld_wq(pb["it"] + 3)
                            loads(pb["it"] + 2)
                            qproj(pb["it"] + 2)
                while deferred:
                    deferred.pop(0)()
                wpipe = LNPipe(0, 1, True)
                for i in range(NT):
                    def fo(e, i=i):
                        for half in range(2):
                            for cc in range(8):
                                ins = e.matmul(PSD[:, half * 512:(half + 1) * 512], lhsT=OT[:, cc, i * 128:(i + 1) * 128],
                                               rhs=Wo[:, cc, half * 512:(half + 1) * 512], start=(cc == 0), stop=(cc == 7))
                        return ins
                    p.op("pe", fo, reads=["OT%d_%d" % (hp, h) for hp in range(8) for h in range(2)] + ["Wo"], writes=kD)
                    wpipe.push(i, PSD[:, :], kD)
                wpipe.flush()
                p.phase_end()

        def run_unit(s, c):
            load_unit(s, c)
            for l in range(DEPTH):
                load_ln(l)
                if l < 2:
                    pool_phase(l, c, True)
                else:
                    attn_phase(l - 2, c)
                if stop == "ln1_%d" % l:
                    return
                ffn_phase(l, c, l in (1, 2))
                if stop == "ln2_%d" % l:
                    return
                if l == 1:
                    kv_phase(c)
                    if stop == "kv":
                        return

        for s in range(nseq):
            for c in range(2):
                run_unit(s, c)
                store_unit(s, c)
                p.phase_end()
        p.finish()
        print("instructions emitted:", p.ninstr, "semaphores:", p.nsem)
    return nc


def host_constants():
    c = {}
    c["c_ident"] = np.eye(128, dtype=np.float32)
    pb = np.zeros((128, 12, 128), np.float32)
    s_ = np.arange(128)[:, None]
    t_ = np.arange(128)[None, :]
    for wi, w in enumerate(POOL_W):
        band = ((t_ - s_) >= 0) & ((t_ - s_) < w)
        eye = (s_ == t_).astype(np.float32)
        pb[:, wi * 3 + 0, :] = band.astype(np.float32) / w - eye
        pb[:, wi * 3 + 1, :] = ((t_ - s_ + 128) < w).astype(np.float32) / w
        cnt = np.minimum(t_ + 1, w).astype(np.float32)
        pb[:, wi * 3 + 2, :] = band.astype(np.float32) / cnt - eye
    c["c_poolB"] = pb
    cur = (s_ <= t_).astype(np.float32)
    prev = (s_ >= t_).astype(np.float32)
    m = np.zeros((128, 3, 512), np.float32)
    m[:, 0, :] = np.concatenate([cur, prev, cur, prev], axis=1)
    m[:, 1, :] = np.tile(cur[:, 0:64], (1, 8))
    m[:, 2, :] = np.tile(cur[:, 64:128], (1, 8))
    c["c_masks"] = m
    pidx = np.arange(128)
    km = np.zeros((128, 2), np.float32)
    for h in range(2):
        km[:, h] = (((pidx // 32) % 2) == h).astype(np.float32)
    c["c_km"] = km
    inv_freq = (np.float32(10000.0) ** (-(np.arange(0, 64, 2, dtype=np.float32)) / np.float32(64))).astype(np.float32)
    ang = (np.arange(S, dtype=np.float32)[:, None] * inv_freq[None, :]).astype(np.float32)
    cosv = np.cos(ang.astype(np.float64)).astype(np.float32).T
    sinv = np.sin(ang.astype(np.float64)).astype(np.float32).T
    i_of_p = pidx % 32
    a_of_p = pidx // 64
    c["c_cos"] = np.ascontiguousarray(cosv[i_of_p, :])
    sgn = np.where(a_of_p == 0, 1.0, -1.0).astype(np.float32)[:, None]
    c["c_sin"] = np.ascontiguousarray(sinv[i_of_p, :] * sgn)
    return c


def host_layouts(inp):
    o = {}
    f = lambda a: np.ascontiguousarray(a, dtype=np.float32)
    o["pw_l"] = f(inp["pool_w"].reshape(2, 4, 2, 128, 256).transpose(0, 3, 1, 2, 4))
    o["pool_scale"] = f(inp["pool_scale"])
    wq = inp["w_q"].reshape(2, 8, 128, 3, 8, 2, 2, 32)
    o["wq_l"] = f(wq.transpose(0, 3, 4, 2, 1, 6, 5, 7).reshape(2, 3, 8, 128, 8, 128))
    wk = inp["w_kv"][:, :3072].reshape(8, 128, 3, 8, 2, 2, 32)
    o["wk_l"] = f(wk.transpose(2, 3, 1, 0, 5, 4, 6).reshape(3, 8, 128, 8, 128))
    wv = inp["w_kv"][:, 3072:].reshape(8, 128, 3, 1024)
    o["wv_l"] = f(wv.transpose(2, 1, 0, 3))
    o["wo_l"] = f(inp["w_o"].reshape(2, 8, 128, 1024).transpose(0, 2, 1, 3))
    o["wg_l"] = f(inp["ffn_w_gate"].reshape(4, 8, 128, NJ, 128).transpose(0, 3, 2, 1, 4))
    o["wu_l"] = f(inp["ffn_w_up"].reshape(4, 8, 128, NJ, 128).transpose(0, 3, 2, 1, 4))
    o["wd_l"] = f(inp["ffn_w_down"].reshape(4, NJ, 128, 1024).transpose(0, 2, 1, 3))
    o["cw_l"] = f(inp["ffn_conv_w"].reshape(4, 3, NJ, 128).transpose(0, 3, 2, 1))
    o["cb_l"] = f(inp["ffn_conv_b"].reshape(4, NJ, 128).transpose(0, 2, 1))
    for n in ("ln1_g", "ln1_b", "ln2_g", "ln2_b"):
        o[n] = f(inp[n])
    return o


_CACHE = {}


def kernel(**inputs):
    x = np.ascontiguousarray(inputs["x"], dtype=np.float32)
    B = x.shape[0]
    per = B // N_CORES
    shared = host_layouts(inputs)
    shared.update(host_constants())
    if "nc" not in _CACHE:
        _CACHE["nc"] = build(nseq=per)
    nc = _CACHE["nc"]
    in_maps = []
    for ci in range(N_CORES):
        m = dict(shared)
        m["x"] = x[ci * per:(ci + 1) * per]
        in_maps.append(m)
    res = run_bass_kernel_spmd(nc, in_maps, core_ids=list(range(N_CORES)))
    out = np.concatenate([np.asarray(r["out"]) for r in res.results], axis=0)
    return out.astype(np.float32)
```

```python
import contextlib
import math
import numpy as np
import concourse.bass as bass
import concourse.mybir as mybir
from concourse.bass_utils import run_bass_kernel_spmd

F32 = mybir.dt.float32
BF16 = mybir.dt.bfloat16
AF = mybir.ActivationFunctionType
ALU = mybir.AluOpType

D = 1024
S = 2048
UT = 1024
NT = UT // 128
TC = 512
DFF = 2816
NJ = DFF // 128
DEPTH = 4
ALPHA = (2.0 * DEPTH) ** 0.25
EPS = 1e-5
POOL_W = (2, 4, 8, 16)
GROUPS = ((128, 1), (512, 4), (2048, 16))
SQC = math.sqrt(0.044715)
GELU_S = 2.0 * math.sqrt(2.0 / math.pi)
N_CORES = 8

ENGS = ("pe", "act", "dve", "pool", "sp")
SEM_ROT = 30000


class Prog:
    def __init__(self, nc, st):
        self.nc = nc
        self.st = st
        self.E = {"pe": nc.tensor, "act": nc.scalar, "dve": nc.vector, "pool": nc.gpsimd, "sp": nc.sync}
        self.nsem = 0
        self.esem = {}
        self.ecnt = {}
        for e in ENGS:
            self._new_esem(e)
        self.dsem = {}
        self.dcnt = {}
        self.waited = {e: {} for e in ENGS}
        self.last_w = {}
        self.readers = {}
        self.last_tok = {e: None for e in ENGS}
        self.dma_toks = {}
        self.pending = {e: [] for e in ENGS}
        self.ninstr = 0
        self.alias_toks = []
        self.seen = None

    def _new_sem(self, name):
        s = self.st.enter_context(self.nc.semaphore(name))
        self.nsem += 1
        return s

    def _new_esem(self, e):
        self.esem[e] = (self._new_sem("e_%s_%d" % (e, self.nsem)), self.nsem)
        self.ecnt[e] = 0

    def _wait(self, e, toks):
        w = self.waited[e]
        eng = self.E[e]
        for t in toks:
            sem, sid, cnt, _ = t
            if w.get(sid, 0) >= cnt:
                continue
            eng.wait_ge(sem, cnt)
            w[sid] = cnt
            self.ninstr += 1

    def _deps(self, eq, ecmp, reads, writes):
        toks = []
        lw = self.last_w
        rd = self.readers
        for k in reads:
            t = lw.get(k)
            if t is not None and not (ecmp == "pe" and t[3] == "pe"):
                toks.append(t)
        same_ok = ecmp in ("pe", "act", "dve")
        for k in writes:
            t = lw.get(k)
            if t is not None and (not same_ok or t[3] != ecmp):
                toks.append(t)
            for t in rd.get(k, ()):
                if not same_ok or t[3] != ecmp:
                    toks.append(t)
        if self.pending[eq]:
            toks.extend(self.pending[eq])
            self.pending[eq] = []
        if self.seen is not None:
            fresh = False
            for k in reads:
                if k not in self.seen:
                    self.seen.add(k)
                    fresh = fresh or not self._persistent(k)
            for k in writes:
                if k not in self.seen:
                    self.seen.add(k)
                    fresh = fresh or not self._persistent(k)
            if fresh:
                toks.extend(self.alias_toks)
        return toks

    def _record(self, tok, reads, writes):
        rd = self.readers
        for k in reads:
            l = rd.get(k)
            if l is None:
                rd[k] = [tok]
            else:
                l.append(tok)
        for k in writes:
            self.last_w[k] = tok
            rd[k] = []

    def op(self, e, fn, reads=(), writes=()):
        self._wait(e, self._deps(e, e, reads, writes))
        if self.ecnt[e] >= SEM_ROT:
            self._new_esem(e)
        ins = fn(self.E[e])
        sem, sid = self.esem[e]
        self.ecnt[e] += 1
        ins.then_inc(sem, 1)
        tok = (sem, sid, self.ecnt[e], e)
        self.last_tok[e] = tok
        self._record(tok, reads, writes)
        self.ninstr += 1
        return tok

    def dma(self, e, fn, key, n=1, reads=(), writes=()):
        self._wait(e, self._deps(e, None, reads, writes))
        if key not in self.dsem:
            self.dsem[key] = (self._new_sem("d_%s" % key), self.nsem)
            self.dcnt[key] = 0
        sem, sid = self.dsem[key]
        fn(self.E[e], sem)
        self.dcnt[key] += 16 * n
        tok = (sem, sid, self.dcnt[key], None)
        self.dma_toks[sid] = tok
        self._record(tok, reads, writes)
        self.ninstr += n
        return tok

    def barrier(self):
        toks = [t for t in self.last_tok.values() if t is not None] + list(self.dma_toks.values())
        for e in ENGS:
            self.pending[e] = list(toks)
        self.last_w = {}
        self.readers = {}
        self.dma_toks = {}
        self.alias_toks = []
        self.seen = None

    PERSIST = ("X", "halo", "cy", "ident", "poolB", "masks", "km", "eps", "lnp", "sta", "stb", "mv", "sd",
               "rs", "nm", "A0", "A1", "B0", "B1", "C0", "C1", "D0", "D1")

    @staticmethod
    def _persistent(k):
        if k in ("A0", "A1", "B0", "B1", "C0", "C1", "D0", "D1", "ident", "poolB", "masks", "km", "eps", "pws", "cw"):
            return True
        for pre in ("X", "halo", "cy", "lnp", "sta", "stb", "mv", "sd", "rs", "nm"):
            if k.startswith(pre) and (len(k) == len(pre) or k[len(pre)].isdigit() or k[len(pre)] in "bT"):
                return True
        return False

    def phase_end(self):
        best = {}
        for t in getattr(self, "alias_toks", []) or []:
            if t[1] not in best or best[t[1]][2] < t[2]:
                best[t[1]] = t
        dead = []
        for k, t in self.last_w.items():
            if not self._persistent(k):
                dead.append(k)
                if t[1] not in best or best[t[1]][2] < t[2]:
                    best[t[1]] = t
        for k, l in self.readers.items():
            if not self._persistent(k):
                if k not in self.last_w:
                    dead.append(k)
                for t in l:
                    if t[1] not in best or best[t[1]][2] < t[2]:
                        best[t[1]] = t
        for k in dead:
            self.last_w.pop(k, None)
            self.readers.pop(k, None)
        self.alias_toks = list(best.values())
        self.seen = set(self.last_w.keys()) | set(self.readers.keys())

    def finish(self):
        toks = [t for t in self.last_tok.values() if t is not None] + list(self.dma_toks.values())
        self._wait("sp", toks)


def build(nseq=4, stop=None, dump_kv=False):
    nc = bass.Bass("TRN2", target_bir_lowering=False)

    def din(name, shape):
        return nc.dram_tensor(name, list(shape), F32, kind="ExternalInput").ap()

    x_d = din("x", [nseq, S, D])
    pw_d = din("pw_l", [2, 128, 4, 2, 256])
    psc_d = din("pool_scale", [2, D])
    wq_d = din("wq_l", [2, 3, 8, 128, 8, 128])
    wk_d = din("wk_l", [3, 8, 128, 8, 128])
    wv_d = din("wv_l", [3, 128, 8, 1024])
    wo_d = din("wo_l", [2, 128, 8, 1024])
    wg_d = din("wg_l", [4, NJ, 128, 8, 128])
    wu_d = din("wu_l", [4, NJ, 128, 8, 128])
    wd_d = din("wd_l", [4, 128, NJ, 1024])
    cw_d = din("cw_l", [4, 128, NJ, 3])
    cb_d = din("cb_l", [4, 128, NJ])
    ln_d = [din(n, [4, D]) for n in ("ln1_g", "ln1_b", "ln2_g", "ln2_b")]
    ident_d = din("c_ident", [128, 128])
    poolB_d = din("c_poolB", [128, 12, 128])
    masks_d = din("c_masks", [128, 3, 512])
    km_d = din("c_km", [128, 2])
    cos_d = din("c_cos", [128, S])
    sin_d = din("c_sin", [128, S])
    out_d = nc.dram_tensor("out", [nseq, S, D], F32, kind="ExternalOutput").ap()
    kvkind = "ExternalOutput" if dump_kv else "Internal"
    kt_hbm = nc.dram_tensor("kt_hbm", [3, 8, 2, 128, S], BF16, kind=kvkind).ap()
    v_hbm = nc.dram_tensor("v_hbm", [S, 3, 16, 128], BF16, kind=kvkind).ap()

    uid = [0]

    with contextlib.ExitStack() as st:
        p = Prog(nc, st)

        def sbuf(stack, name, shape, dt=F32):
            uid[0] += 1
            return stack.enter_context(nc.sbuf_tensor("%s_%d" % (name, uid[0]), list(shape), dt))

        X = sbuf(st, "X", [128, NT, D])
        Xb = sbuf(st, "Xb", [128, NT, D], BF16)
        XT = sbuf(st, "XT", [128, 8, UT], BF16)
        halo = sbuf(st, "halo", [128, 2, D], BF16)
        carry = sbuf(st, "carry", [128, 4, NJ, 2])
        ident = sbuf(st, "ident", [128, 128], BF16)
        poolB = sbuf(st, "poolB", [128, 12, 128], BF16)
        masks = sbuf(st, "masks", [128, 3, 512], BF16)
        km = sbuf(st, "km", [128, 2])
        epsT = sbuf(st, "epsT", [128, 1])
        lnp = sbuf(st, "lnp", [128, 4, D])
        stats = sbuf(st, "stats", [128, 4, 2, 6])
        mv = sbuf(st, "mv", [128, 4, 2])
        sd = sbuf(st, "sd", [128, 4, 4])
        pwsP = sbuf(st, "pwsP", [128, 2, 4, 2, 256], BF16)
        cwP = sbuf(st, "cwP", [128, 4, NJ, 3])
        cbP = sbuf(st, "cbP", [128, 4, NJ])
        PSA = st.enter_context(nc.psum_tensor("PSA", [128, 1024], F32))
        PSB = st.enter_context(nc.psum_tensor("PSB", [128, 1024], F32))
        PSD = st.enter_context(nc.psum_tensor("PSD", [128, 1024], F32))
        PSC = [st.enter_context(nc.psum_tensor("PSC%d" % i, [128, 512], F32)) for i in range(2)]
        TRB = [PSC[i][:, :].bitcast(BF16).rearrange("p (k n) -> p k n", n=128) for i in range(2)]
        kA = ["A0", "A1"]
        kB = ["B0", "B1"]
        kD = ["D0", "D1"]
        kC = ["C0", "C1"]

        def ld_consts(e, s):
            e.dma_start(out=ident[:], in_=ident_d[:, :]).then_inc(s, 16)
            e.dma_start(out=poolB[:], in_=poolB_d[:, :, :]).then_inc(s, 16)
            e.dma_start(out=masks[:], in_=masks_d[:, :, :]).then_inc(s, 16)
        p.dma("pool", ld_consts, "const", n=3, writes=["ident", "poolB", "masks"])
        p.dma("sp", lambda e, s: e.dma_start(out=km[:], in_=km_d[:, :]).then_inc(s, 16), "const2", writes=["km"])
        p.op("dve", lambda e: e.memset(epsT[:], EPS), writes=["eps"])

        par_state = {"ln": 0, "xt": 0}

        with contextlib.ExitStack() as ph0:
            pwf0 = sbuf(ph0, "pwf0", [128, 2, 4, 2, 256], BF16)
            psc0 = sbuf(ph0, "psc0", [128, 2, D])

            def ld0(e, s):
                for l_ in range(2):
                    e.dma_start(out=pwf0[:, l_, :, :, :], in_=pw_d[l_]).then_inc(s, 16)
            p.dma("pool", ld0, "pwf", n=2, writes=["pwf"])

            def ld1(e, s):
                for l_ in range(2):
                    e.dma_start(out=psc0[:, l_, :], in_=psc_d[l_, :].partition_broadcast(128)).then_inc(s, 16)
                for l_ in range(4):
                    e.dma_start(out=cwP[:, l_, :, :], in_=cw_d[l_]).then_inc(s, 16)
                    e.dma_start(out=cbP[:, l_, :], in_=cb_d[l_]).then_inc(s, 16)
            p.dma("sp", ld1, "psc", n=10, writes=["psc", "cw"])
            for l_ in range(2):
                for g in range(4):
                    for kc in range(2):
                        p.op("pool", (lambda l_, g, kc: lambda e: e.tensor_tensor(
                            pwsP[:, l_, g, kc, :], pwf0[:, l_, g, kc, :], psc0[:, l_, g * 256:(g + 1) * 256], ALU.mult))(l_, g, kc),
                            reads=["pwf", "psc"], writes=["pws"])
            p.phase_end()

        ln_ctr = [0]

        class LNPipe:
            NST = 5

            def __init__(self, gi, bi, want_xt):
                self.items = []
                self.gi, self.bi, self.want_xt = gi, bi, want_xt

            def _stage(self, k, item):
                i, ps_ap, pskeys, gidx = item
                par = gidx % 4
                gi, bi = self.gi, self.bi
                Xi = X[:, i, :]
                kx = "X%d" % i
                sa, sb_, smv, ssd, srs, snm = ("sta%d" % par, "stb%d" % par, "mv%d" % par, "sd%d" % par,
                                               "rs%d" % par, "nm%d" % par)
                if k == 0:
                    p.op("dve", lambda e: e.scalar_tensor_tensor(Xi, Xi, ALPHA, ps_ap, ALU.mult, ALU.add),
                         reads=list(pskeys) + [kx], writes=[kx])
                    p.op("dve", lambda e: e.bn_stats(stats[:, par, 0, :], X[:, i, 0:512]), reads=[kx], writes=[sa])
                    p.op("dve", lambda e: e.bn_stats(stats[:, par, 1, :], X[:, i, 512:1024]), reads=[kx], writes=[sb_])
                    p.op("dve", lambda e: e.bn_aggr(mv[:, par, :], stats[:, par, :, :]), reads=[sa, sb_], writes=[smv])
                elif k == 1:
                    p.op("act", lambda e: e.activation(sd[:, par, 0:1], mv[:, par, 1:2], AF.Sqrt, bias=epsT[:], scale=1.0),
                         reads=[smv, "eps"], writes=[ssd])
                    p.op("dve", lambda e: e.reciprocal(sd[:, par, 1:2], sd[:, par, 0:1]), reads=[ssd], writes=[srs])
                    p.op("dve", lambda e: e.tensor_scalar(sd[:, par, 2:3], mv[:, par, 0:1], sd[:, par, 1:2], -1.0,
                                                          ALU.mult, ALU.mult), reads=[smv, srs], writes=[snm])
                elif k == 2:
                    p.op("act", lambda e: e.activation(Xi, Xi, AF.Identity, bias=sd[:, par, 2:3], scale=sd[:, par, 1:2]),
                         reads=[kx, srs, snm], writes=[kx])
                    p.op("dve", lambda e: e.tensor_tensor(Xi, Xi, lnp[:, gi, :], ALU.mult), reads=[kx, "lnp%d" % gi], writes=[kx])
                    p.op("dve", lambda e: e.tensor_tensor(Xi, Xi, lnp[:, bi, :], ALU.add), reads=[kx, "lnp%d" % bi], writes=[kx])
                elif k == 3:
                    p.op("act", lambda e: e.activation(Xb[:, i, :], Xi, AF.Identity), reads=[kx], writes=["Xb%d" % i])
                elif k == 4 and self.want_xt:
                    xt_tile(i)

            def _step(self, s):
                n = len(self.items)
                for k in range(self.NST):
                    j = s - k
                    if 0 <= j < n:
                        self._stage(k, self.items[j])

            def push(self, i, ps_ap, pskeys):
                self.items.append((i, ps_ap, pskeys, ln_ctr[0]))
                ln_ctr[0] += 1
                self._step(len(self.items) - 1)

            def flush(self):
                n = len(self.items)
                for s in range(n, n + self.NST - 1):
                    self._step(s)
                self.items = []

        def xt_tile(i):
            par = par_state["xt"]
            par_state["xt"] ^= 1
            trp = TRB[par]

            def f(e):
                for k in range(8):
                    ins = e.transpose(trp[:, k, :], Xb[:, i, k * 128:(k + 1) * 128], ident[:])
                return ins
            p.op("pe", f, reads=["Xb%d" % i, "ident"], writes=[kC[par]])
            p.op("act", lambda e: e.activation(XT[:, :, i * 128:(i + 1) * 128], trp, AF.Identity),
                 reads=[kC[par]], writes=["XT%d" % i])

        def load_ln1(l):
            def f(e, s):
                for q in range(2):
                    e.dma_start(out=lnp[:, q, :], in_=ln_d[q][l, :].partition_broadcast(128)).then_inc(s, 16)
            p.dma("sp", f, "lnpA", n=2, writes=["lnp0", "lnp1"])

        def load_ln2(l):
            def f(e, s):
                for q in range(2, 4):
                    e.dma_start(out=lnp[:, q, :], in_=ln_d[q][l, :].partition_broadcast(128)).then_inc(s, 16)
            p.dma("sp", f, "lnpB", n=2, writes=["lnp2", "lnp3"])

        def load_unit(s, c):
            for i in range(NT):
                r0 = c * UT + i * 128
                p.dma("sp", (lambda i, r0: lambda e, sm: e.dma_start(out=X[:, i, :], in_=x_d[s, r0:r0 + 128, :]).then_inc(sm, 16))(i, r0),
                      "ldx%d" % i, writes=["X%d" % i])
            for i in range(NT):
                if i % 2 == 0:
                    p.op("act", (lambda i: lambda e: e.activation(Xb[:, i, :], X[:, i, :], AF.Identity))(i),
                         reads=["X%d" % i], writes=["Xb%d" % i])
                else:
                    p.op("dve", (lambda i: lambda e: e.tensor_copy(Xb[:, i, :], X[:, i, :]))(i),
                         reads=["X%d" % i], writes=["Xb%d" % i])
            if c == 0:
                p.op("pool", lambda e: e.memset(carry[:], 0.0),
                     writes=["cy%d_%d" % (l, j) for l in range(4) for j in range(NJ)])

        def store_unit(s, c):
            for i in range(NT):
                r0 = c * UT + i * 128
                p.dma("sp", (lambda i, r0: lambda e, sm: e.dma_start(out=out_d[s, r0:r0 + 128, :], in_=X[:, i, :]).then_inc(sm, 16))(i, r0),
                      "stx%d" % i, reads=["X%d" % i])

        def pool_phase(l, c, want_xt):
            with contextlib.ExitStack() as ph:
                pmT = sbuf(ph, "pmT", [128, 2, 8, 128], BF16)
                if c == 0:
                    p.op("act", lambda e: e.activation(halo[:, l, :], Xb[:, NT - 1, :], AF.Identity),
                         reads=["Xb%d" % (NT - 1)], writes=["halo%d" % l])
                pwskeys = ["pws"]

                def pm(i):
                    ps = PSA if i % 2 == 0 else PSB
                    keys = kA if i % 2 == 0 else kB
                    first = (c == 0 and i == 0)
                    hasprev = not first
                    rk = ["Xb%d" % i, "poolB"]
                    if hasprev:
                        rk.append("Xb%d" % (i - 1) if i > 0 else "halo%d" % l)

                    def f(e):
                        for cc in range(8):
                            g = cc // 2
                            bsel = poolB[:, g * 3 + (2 if first else 0), :]
                            ins = e.matmul(ps[:, cc * 128:(cc + 1) * 128], lhsT=Xb[:, i, cc * 128:(cc + 1) * 128],
                                           rhs=bsel, start=True, stop=not hasprev)
                            if hasprev:
                                src = Xb[:, i - 1, cc * 128:(cc + 1) * 128] if i > 0 else halo[:, l, cc * 128:(cc + 1) * 128]
                                ins = e.matmul(ps[:, cc * 128:(cc + 1) * 128], lhsT=src, rhs=poolB[:, g * 3 + 1, :],
                                               start=False, stop=True)
                        return ins
                    p.op("pe", f, reads=rk, writes=keys)
                    p.op("act", lambda e: e.activation(pmT[:, i % 2, :, :], ps[:, :].rearrange("p (k n) -> p k n", n=128),
                                                       AF.Identity), reads=keys, writes=["pmT%d" % (i % 2)])

                def rest(i):
                    def f(e):
                        for g in range(4):
                            for kc in range(2):
                                ins = e.matmul(PSD[:, g * 256:(g + 1) * 256], lhsT=pmT[:, i % 2, 2 * g + kc, :],
                                               rhs=pwsP[:, l, g, kc, :], start=(kc == 0), stop=(kc == 1))
                        return ins
                    p.op("pe", f, reads=["pmT%d" % (i % 2)] + pwskeys, writes=kD)

                pipe = LNPipe(0, 1, want_xt)
                pm(0)
                for i in range(NT):
                    if i + 1 < NT:
                        pm(i + 1)
                    rest(i)
                    pipe.push(i, PSD[:, :], kD)
                pipe.flush()
                p.phase_end()

        def ffn_phase(l, c, want_xt):
            with contextlib.ExitStack() as ph:
                hT = sbuf(ph, "hT", [128, NJ, TC], BF16)
                Wd = sbuf(ph, "Wd", [128, NJ, D], BF16)
                wg = [sbuf(ph, "wg%d" % i, [128, 8, 128], BF16) for i in range(4)]
                wu = [sbuf(ph, "wu%d" % i, [128, 8, 128], BF16) for i in range(4)]
                gbuf = sbuf(ph, "gbuf", [128, 2, TC + 2])
                cbf = sbuf(ph, "cbf", [128, 3, TC])
                sq = sbuf(ph, "sq", [128, 2, TC])
                sg = sbuf(ph, "sg", [128, 2, TC])
                cu = sbuf(ph, "cu", [128, 2, TC])
                cw = cwP[:, l, :, :]
                cbias = cbP[:, l, :]

                def ld_w(q):
                    slot = q % 4
                    j = q % NJ

                    def f(e, s):
                        e.dma_start(out=wg[slot][:], in_=wg_d[l, j]).then_inc(s, 16)
                        e.dma_start(out=wu[slot][:], in_=wu_d[l, j]).then_inc(s, 16)
                    p.dma("pool", f, "wgu%d" % slot, n=2, writes=["wgu%d" % slot])

                ld_w(0)
                ld_w(1)
                bounds = [0, 6, 12, 17, NJ]

                def ldwd(e, s):
                    for q in range(4):
                        a, b = bounds[q], bounds[q + 1]
                        e.dma_start(out=Wd[:, a:b, :], in_=wd_d[l][:, a:b, :]).then_inc(s, 16)
                NQ = 2 * NJ

                def st_pe(q):
                    ch, j = divmod(q, NJ)
                    par = q % 2
                    slot = q % 4
                    gp = PSA[:, par * 512:(par + 1) * 512]
                    up = PSB[:, par * 512:(par + 1) * 512]
                    xtk = ["XT%d" % (ch * 4 + t) for t in range(4)]

                    def fg(e):
                        for k in range(8):
                            ins = e.matmul(gp, lhsT=wg[slot][:, k, :], rhs=XT[:, k, ch * TC:(ch + 1) * TC],
                                           start=(k == 0), stop=(k == 7))
                        return ins

                    def fu(e):
                        for k in range(8):
                            ins = e.matmul(up, lhsT=wu[slot][:, k, :], rhs=XT[:, k, ch * TC:(ch + 1) * TC],
                                           start=(k == 0), stop=(k == 7))
                        return ins
                    p.op("pe", fg, reads=["wgu%d" % slot] + xtk, writes=[kA[par]])
                    p.op("pe", fu, reads=["wgu%d" % slot] + xtk, writes=[kB[par]])

                def st_s1(q):
                    ch, j = divmod(q, NJ)
                    par = q % 2
                    c3 = q % 3
                    gp = PSA[:, par * 512:(par + 1) * 512]
                    gk = kA[par]
                    cbp = cbf[:, c3, :]
                    kcy = "cy%d_%d" % (l, j)
                    kgh, kgb, kc_ = "gbh%d" % par, "gbb%d" % par, "cb%d" % c3
                    p.op("pool", lambda e: e.tensor_copy(gbuf[:, par, 0:2], carry[:, l, j, :]), reads=[kcy], writes=[kgh])
                    p.op("act", lambda e: e.activation(gbuf[:, par, 2:TC + 2], gp, AF.Identity), reads=[gk], writes=[kgb])
                    p.op("act", lambda e: e.activation(cbp, gp, AF.Identity, bias=cbias[:, j:j + 1], scale=cw[:, j, 2:3]),
                         reads=[gk, "cw"], writes=[kc_])
                    p.op("pool", lambda e: e.tensor_copy(carry[:, l, j, :], gbuf[:, par, TC:TC + 2]), reads=[kgb], writes=[kcy])
                    p.op("dve", lambda e: e.scalar_tensor_tensor(cbp, gbuf[:, par, 1:TC + 1], cw[:, j, 1:2], cbp, ALU.mult, ALU.add),
                         reads=[kgb, kgh, kc_, "cw"], writes=[kc_])
                    p.op("dve", lambda e: e.scalar_tensor_tensor(cbp, gbuf[:, par, 0:TC], cw[:, j, 0:1], cbp, ALU.mult, ALU.add),
                         reads=[kgb, kgh, kc_, "cw"], writes=[kc_])

                def st_s2(q):
                    par = q % 2
                    c3 = q % 3
                    cbp = cbf[:, c3, :]
                    sqp = sq[:, par, :]
                    kc_, ksq = "cb%d" % c3, "sq%d" % par
                    p.op("act", lambda e: e.activation(sqp, cbp, AF.Square, scale=SQC), reads=[kc_], writes=[ksq])
                    p.op("dve", lambda e: e.scalar_tensor_tensor(sqp, sqp, 1.0, cbp, ALU.add, ALU.mult), reads=[ksq, kc_], writes=[ksq])

                def st_d4(q):
                    par = q % 2
                    c3 = q % 3
                    up = PSB[:, par * 512:(par + 1) * 512]
                    p.op("dve", lambda e: e.tensor_tensor(cu[:, par, :], cbf[:, c3, :], up, ALU.mult),
                         reads=["cb%d" % c3, kB[par]], writes=["cu%d" % par])

                def st_s3(q):
                    ch, j = divmod(q, NJ)
                    par = q % 2
                    p.op("act", lambda e: e.activation(sg[:, par, :], sq[:, par, :], AF.Sigmoid, scale=GELU_S),
                         reads=["sq%d" % par], writes=["sg%d" % par])
                    p.op("dve", lambda e: e.tensor_tensor(hT[:, j, :], cu[:, par, :], sg[:, par, :], ALU.mult),
                         reads=["cu%d" % par, "sg%d" % par], writes=["hT%d" % j])

                def step(t):
                    if 0 <= t - 2 < NQ:
                        st_d4(t - 2)
                    if t < NQ:
                        if t + 2 < NQ:
                            ld_w(t + 2)
                        if t == 3:
                            p.dma("pool", ldwd, "Wd", n=4, writes=["Wd"])
                        st_pe(t)
                        st_s1(t)
                    if 0 <= t - 1 < NQ:
                        st_s2(t - 1)
                    if 0 <= t - 2 < NQ:
                        st_s3(t - 2)

                def down(ch, t):
                    i = ch * 4 + t

                    def f(e):
                        for half in range(2):
                            for j in range(NJ):
                                ins = e.matmul(PSD[:, half * 512:(half + 1) * 512], lhsT=hT[:, j, t * 128:(t + 1) * 128],
                                               rhs=Wd[:, j, half * 512:(half + 1) * 512], start=(j == 0), stop=(j == NJ - 1))
                        return ins
                    p.op("pe", f, reads=["hT%d" % j for j in range(NJ)] + ["Wd"], writes=kD)

                def downs(ch):
                    pipe = LNPipe(2, 3, want_xt)
                    for t in range(4):
                        down(ch, t)
                        pipe.push(ch * 4 + t, PSD[:, :], kD)
                    pipe.flush()

                for t in range(NQ + 2):
                    step(t)
                    if t == NJ + 1:
                        downs(0)
                downs(1)
                p.phase_end()

        def rope(ps_ap, pskey, cosap, sinap, ra, ru, par, out_fn):
            kra, kru0, kru1 = "ra%d" % par, "ru0_%d" % par, "ru1_%d" % par
            p.op("dve", lambda e: e.tensor_tensor(ra[:, par, :], ps_ap, cosap, ALU.mult), reads=[pskey, "rope"], writes=[kra])
            p.op("dve", lambda e: e.tensor_tensor(ru[64:128, par, :], ps_ap[0:64, :], sinap[0:64, :], ALU.mult),
                 reads=[pskey, "rope"], writes=[kru1])
            p.op("dve", lambda e: e.tensor_tensor(ru[0:64, par, :], ps_ap[64:128, :], sinap[64:128, :], ALU.mult),
                 reads=[pskey, "rope"], writes=[kru0])
            out_fn(ra[:, par, :], ru[:, par, :], [kra, kru0, kru1])

        def kv_phase(c):
            with contextlib.ExitStack() as ph:
                cosU = sbuf(ph, "cosU", [128, UT])
                sinU = sbuf(ph, "sinU", [128, UT])
                wvb = [sbuf(ph, "wv%d" % i, [128, 8, D], BF16) for i in range(2)]
                wk = [sbuf(ph, "wk%d" % i, [128, 8, 128], BF16) for i in range(4)]
                kst = [sbuf(ph, "kst%d" % i, [128, 2, UT], BF16) for i in range(2)]
                vst = [sbuf(ph, "vst%d" % i, [128, 16, 128], BF16) for i in range(2)]
                ra = sbuf(ph, "ra", [128, 2, TC])
                ru = sbuf(ph, "ru", [128, 2, TC])
                rs = sbuf(ph, "rs", [128, 2, TC])

                def ldr(e, s):
                    e.dma_start(out=cosU[:], in_=cos_d[:, c * UT:(c + 1) * UT]).then_inc(s, 16)
                    e.dma_start(out=sinU[:], in_=sin_d[:, c * UT:(c + 1) * UT]).then_inc(s, 16)
                p.dma("sp", ldr, "rope", n=2, writes=["rope"])
                for i in range(2):
                    p.op("pool", (lambda i: lambda e: e.memset(vst[i][:], 1.0))(i), writes=["vst%d" % i, "vst%db" % i])
                xtall = ["XT%d" % i for i in range(NT)]
                cnt = 0
                vcnt = 0
                def ld_wk(n):
                    g_, hp_ = divmod(n, 8)
                    sl_ = n % 4
                    p.dma("pool", lambda e, s: e.dma_start(out=wk[sl_][:], in_=wk_d[g_, hp_]).then_inc(s, 16),
                          "wk%d" % sl_, writes=["wk%d" % sl_])

                def ld_wv(g_):
                    p.dma("pool", lambda e, s: e.dma_start(out=wvb[g_ % 2][:], in_=wv_d[g_]).then_inc(s, 16),
                          "wv%d" % (g_ % 2), writes=["wv%d" % (g_ % 2)])
                ld_wk(0)
                ld_wk(1)
                ld_wv(0)
                for g in range(3):
                    wv = wvb[g % 2]
                    wvk = "wv%d" % (g % 2)
                    if g + 1 < 3:
                        ld_wv(g + 1)
                    for hp in range(8):
                        sl = cnt % 4
                        ks = cnt % 2
                        if cnt + 2 < 24:
                            ld_wk(cnt + 2)
                        cnt += 1
                        for tc in range(2):
                            par = tc
                            kp = PSA[:, par * 512:(par + 1) * 512]

                            def fk(e, sl=sl, tc=tc, kp=kp):
                                for k in range(8):
                                    ins = e.matmul(kp, lhsT=wk[sl][:, k, :], rhs=XT[:, k, tc * TC:(tc + 1) * TC],
                                                   start=(k == 0), stop=(k == 7))
                                return ins
                            p.op("pe", fk, reads=["wk%d" % sl] + xtall[tc * 4:(tc + 1) * 4], writes=[kA[par]])

                            def fin(ra_ap, ru_ap, keys, sl=ks, tc=tc, par=par):
                                p.op("dve", lambda e: e.tensor_tensor(rs[:, par, :], ra_ap, ru_ap, ALU.add), reads=keys, writes=["krs%d" % par])
                                for h in range(2):
                                    p.op("act", (lambda h: lambda e: e.activation(kst[sl][:, h, tc * TC:(tc + 1) * TC], rs[:, par, :],
                                                                                  AF.Identity, scale=km[:, h:h + 1]))(h),
                                         reads=["krs%d" % par, "km"], writes=["kst%d_%d_%d" % (sl, h, tc)])
                            rope(kp, kA[par], cosU[:, tc * TC:(tc + 1) * TC], sinU[:, tc * TC:(tc + 1) * TC], ra, ru, par, fin)
                        dst = kt_hbm[g, hp, :, :, c * UT:(c + 1) * UT].rearrange("h q t -> q h t")
                        p.dma("sp", (lambda dst, sl: lambda e, s: e.dma_start(out=dst, in_=kst[sl][:]).then_inc(s, 16))(dst, ks),
                              "kst%d" % ks, reads=["kst%d_%d_%d" % (ks, h, tc) for h in range(2) for tc in range(2)])
                    for t in range(NT):
                        vs = vcnt % 2
                        vcnt += 1
                        vp = PSB if vs == 0 else PSD
                        vk = kB if vs == 0 else kD

                        def fv(e, t=t, vp=vp, wv=wv):
                            for half in range(2):
                                for k in range(8):
                                    ins = e.matmul(vp[:, half * 512:(half + 1) * 512], lhsT=XT[:, k, t * 128:(t + 1) * 128],
                                                   rhs=wv[:, k, half * 512:(half + 1) * 512], start=(k == 0), stop=(k == 7))
                            return ins
                        p.op("pe", fv, reads=[wvk, "XT%d" % t], writes=vk)
                        vview = vp[:, :].rearrange("p (h c) -> p h c", c=64)
                        p.op("act", (lambda vs, vview: lambda e: e.activation(vst[vs][:, 0:16:2, 0:64], vview[:, 0:16:2, :], AF.Identity))(vs, vview),
                             reads=vk, writes=["vst%d" % vs])
                        p.op("act", (lambda vs, vview: lambda e: e.activation(vst[vs][:, 1:16:2, 64:128], vview[:, 1:16:2, :], AF.Identity))(vs, vview),
                             reads=vk, writes=["vst%db" % vs])
                        r0 = c * UT + t * 128
                        p.dma("sp", (lambda vs, g, r0: lambda e, s: e.dma_start(out=v_hbm[r0:r0 + 128, g, :, :], in_=vst[vs][:]).then_inc(s, 16))(vs, g, r0),
                              "vst%d" % vs, reads=["vst%d" % vs, "vst%db" % vs])
                p.phase_end()

        def attn_phase(jl, c):
            KL = UT * (c + 1)
            with contextlib.ExitStack() as ph:
                cosU = sbuf(ph, "cosU", [128, UT])
                sinU = sbuf(ph, "sinU", [128, UT])
                OT = sbuf(ph, "OT", [128, 8, UT], BF16)
                Wo = sbuf(ph, "Wo", [128, 8, D], BF16)
                wq = [sbuf(ph, "wq%d" % i, [128, 8, 128], BF16) for i in range(3)]
                QT = [sbuf(ph, "QT%d" % i, [128, UT], BF16) for i in range(2)]
                kt = [sbuf(ph, "kt%d" % i, [128, 2, S], BF16) for i in range(2)]
                vb = [sbuf(ph, "vb%d" % i, [128, 16, 256], BF16) for i in range(2)]
                esb = [sbuf(ph, "esb%d" % i, [128, 512], BF16) for i in range(2)]
                PT = [sbuf(ph, "PT%d" % i, [128, 512], BF16) for i in range(3)]
                rden = sbuf(ph, "rden", [128, UT])
                Osb = sbuf(ph, "Osb", [128, 2, UT])
                ra = sbuf(ph, "ra", [128, 2, TC])
                ru = sbuf(ph, "ru", [128, 2, TC])

                def ldr(e, s):
                    e.dma_start(out=cosU[:], in_=cos_d[:, c * UT:(c + 1) * UT]).then_inc(s, 16)
                    e.dma_start(out=sinU[:], in_=sin_d[:, c * UT:(c + 1) * UT]).then_inc(s, 16)
                p.dma("sp", ldr, "rope", n=2, writes=["rope"])
                p.dma("pool", lambda e, s: e.dma_start(out=Wo[:], in_=wo_d[jl]).then_inc(s, 16), "Wo", writes=["Wo"])
                xtall = ["XT%d" % i for i in range(NT)]
                its = [(hp, g) for hp in range(8) for g in range(3)]

                def ld_wq(it):
                    hp, g = its[it]
                    s3 = it % 3
                    p.dma("pool", lambda e, s: e.dma_start(out=wq[s3][:], in_=wq_d[jl, g, hp]).then_inc(s, 16), "wq%d" % s3, writes=["wq%d" % s3])

                def loads(it):
                    hp, g = its[it]
                    sl = it % 2
                    d = GROUPS[g][1]
                    NB = 16 // d
                    src = kt_hbm[g, hp, :, :, 0:KL].rearrange("h q t -> q h t")
                    p.dma("sp", lambda e, s: e.dma_start(out=kt[sl][:, :, 0:KL], in_=src).then_inc(s, 16), "kt%d" % sl, writes=["kt%d" % sl])
                    vsrc = v_hbm[:, g, 2 * hp:2 * hp + 2, :].rearrange("(n i r) h c -> i r n (h c)", n=NB, i=128, r=d)
                    vdst = vb[sl][:, :, :].rearrange("p (r n) x -> p r n x", r=d, n=NB)

                    def fv(e, s):
                        for r in range(d):
                            if d == 16:
                                rows = 64 * (c + 1)
                                e.dma_start(out=vdst[0:rows, r, 0, :], in_=vsrc[0:rows, r, 0, :]).then_inc(s, 16)
                            else:
                                nbv = NB // 2 * (c + 1)
                                e.dma_start(out=vdst[:, r, 0:nbv, :], in_=vsrc[:, r, 0:nbv, :]).then_inc(s, 16)
                    p.dma("sp", fv, "vb%d" % sl, n=d, writes=["vb%d" % sl])

                def qproj(it):
                    hp, g = its[it]
                    sl = it % 2
                    for tc in range(2):
                        par = tc
                        qp = PSA[:, par * 512:(par + 1) * 512]

                        def fq(e, tc=tc, qp=qp):
                            for k in range(8):
                                ins = e.matmul(qp, lhsT=wq[it % 3][:, k, :], rhs=XT[:, k, tc * TC:(tc + 1) * TC],
                                               start=(k == 0), stop=(k == 7))
                            return ins
                        p.op("pe", fq, reads=["wq%d" % (it % 3)] + xtall[tc * 4:(tc + 1) * 4], writes=[kA[par]])

                        def fin(ra_ap, ru_ap, keys, tc=tc):
                            p.op("dve", lambda e: e.tensor_tensor(QT[sl][:, tc * TC:(tc + 1) * TC], ra_ap, ru_ap, ALU.add),
                                 reads=keys, writes=["QT%d_%d" % (sl, tc)])
                        rope(qp, kA[par], cosU[:, tc * TC:(tc + 1) * TC], sinU[:, tc * TC:(tc + 1) * TC], ra, ru, par, fin)

                batches = []
                for it, (hp, g) in enumerate(its):
                    d = GROUPS[g][1]
                    NB = 16 // d
                    for h in range(2):
                        items = []
                        if d == 16:
                            kc = 64 * (c + 1)
                            for r in range(16):
                                items.append(dict(kc=kc, kstart=r, blk=r * NB, parts=[(0, r, 64)]))
                            groups_ = [items[0:8], items[8:16]]
                            mt = 1 + c
                            slotw = 64
                        else:
                            nblk = 1024 // (128 * d)
                            qlo, qhi = nblk * c, nblk * (c + 1)
                            for r in range(d):
                                for n in range(max(qlo - 1, 0), qhi):
                                    has_cur = qlo <= n < qhi
                                    has_prev = qlo <= n + 1 < qhi
                                    qc = 128 * n * d + r - c * UT
                                    qn = 128 * (n + 1) * d + r - c * UT
                                    if has_cur and has_prev:
                                        parts = [(0, qc, 256)]
                                    elif has_cur:
                                        parts = [(0, qc, 128)]
                                    else:
                                        parts = [(128, qn, 128)]
                                    items.append(dict(kc=128, kstart=128 * n * d + r, blk=r * NB + n, parts=parts))
                            groups_ = [items[i:i + 2] for i in range(0, len(items), 2)]
                            mt = 0
                            slotw = 256
                        for grp in groups_:
                            batches.append(dict(it=it, hp=hp, g=g, h=h, d=d, items=grp, mt=mt, slotw=slotw))

                def qk(bi):
                    b = batches[bi]
                    sl = b["it"] % 2
                    bp = bi % 2
                    Sp = PSC[bp]
                    d, h = b["d"], b["h"]

                    def f(e):
                        for ci, itx in enumerate(b["items"]):
                            kc = itx["kc"]
                            ks = itx["kstart"]
                            base = ci * b["slotw"]
                            for (so, q0, nq) in itx["parts"]:
                                ins = e.matmul(Sp[0:kc, base + so:base + so + nq],
                                               lhsT=kt[sl][:, h, ks:ks + (kc - 1) * d + 1:d],
                                               rhs=QT[sl][:, q0:q0 + (nq - 1) * d + 1:d], start=True, stop=True)
                        return ins
                    p.op("pe", f, reads=["kt%d" % sl, "QT%d_0" % sl, "QT%d_1" % sl], writes=[kC[bp]])
                    rows = b["items"][0]["kc"]
                    p.op("act", lambda e: e.activation(esb[bp][0:rows, :], Sp[0:rows, :], AF.Exp, scale=0.125),
                         reads=[kC[bp]], writes=["esb%d" % bp])
                    p3 = bi % 3
                    p.op("dve", lambda e: e.tensor_tensor(PT[p3][0:rows, :], esb[bp][0:rows, :], masks[0:rows, b["mt"], :], ALU.mult),
                         reads=["esb%d" % bp, "masks"], writes=["PT%d" % p3])

                def pv(bi, last):
                    b = batches[bi]
                    sl = b["it"] % 2
                    bp = bi % 3
                    d, h = b["d"], b["h"]
                    Op = PSB if h == 0 else PSD
                    ok = kB if h == 0 else kD

                    def f(e):
                        nmm = sum(len(itx["parts"]) for itx in b["items"])
                        m = 0
                        for ci, itx in enumerate(b["items"]):
                            kc = itx["kc"]
                            base = ci * b["slotw"]
                            for (so, q0, nq) in itx["parts"]:
                                m += 1
                                ins = e.matmul(Op[:, q0:q0 + (nq - 1) * d + 1:d],
                                               lhsT=vb[sl][0:kc, itx["blk"], h * 128:(h + 1) * 128],
                                               rhs=PT[bp][0:kc, base + so:base + so + nq],
                                               start=False, stop=(last and m == nmm), skip_group_check=True)
                        return ins
                    p.op("pe", f, reads=["vb%d" % sl, "PT%d" % bp], writes=ok)

                deferred = []

                def evac_head(hp, h, more):
                    Op = PSB if h == 0 else PSD
                    ok = kB if h == 0 else kD
                    p.op("act", lambda e: e.activation(Osb[:, h, :], Op[:, :], AF.Identity), reads=ok, writes=["Osb%d" % h])
                    if more:
                        p.op("dve", lambda e: e.memset(Op[:, :], 0.0), writes=ok)
                    if h == 0:
                        nsl, dsl = slice(0, 64), slice(64, 128)
                    else:
                        nsl, dsl = slice(64, 128), slice(0, 64)
                    kr = "rden%d" % h
                    deferred.append(lambda: p.op("act", lambda e: e.activation(rden[nsl, :], Osb[dsl, h, :], AF.Ln),
                                                 reads=["Osb%d" % h], writes=[kr]))
                    deferred.append(lambda: p.op("act", lambda e: e.activation(rden[nsl, :], rden[nsl, :], AF.Exp, scale=-1.0),
                                                 reads=[kr], writes=[kr]))
                    deferred.append(lambda: p.op("dve", lambda e: e.tensor_tensor(OT[nsl, hp, :], Osb[nsl, h, :], rden[nsl, :], ALU.mult),
                                                 reads=["Osb%d" % h, kr], writes=["OT%d_%d" % (hp, h)]))

                ld_wq(0)
                ld_wq(1)
                ld_wq(2)
                loads(0)
                qproj(0)
                loads(1)
                qproj(1)
                p.op("dve", lambda e: e.memset(PSB[:, :], 0.0), writes=kB)
                p.op("dve", lambda e: e.memset(PSD[:, :], 0.0), writes=kD)
                nbt = len(batches)
                last_of = {}
                last_it = {}
                for bi, b in enumerate(batches):
                    last_of[(b["hp"], b["h"])] = bi
                    last_it[b["it"]] = bi
                SKEW = 2
                for bi in range(nbt + SKEW):
                    if bi < nbt:
                        qk(bi)
                    if deferred:
                        deferred.pop(0)()
                    k = bi - SKEW
                    if k >= 0:
                        pb = batches[k]
                        is_last = last_of[(pb["hp"], pb["h"])] == k
                        pv(k, is_last)
                        if is_last:
                            evac_head(pb["hp"], pb["h"], pb["hp"] < 7)
                        if last_it[pb["it"]] == k and pb["it"] + 2 < len(its):
                            if pb["it"] + 3 < len(its):
                                ld_wq(pb["it"] + 3)
                            loads(pb["it"] + 2)
                            qproj(pb["it"] + 2)
                while deferred:
                    deferred.pop(0)()
                wpipe = LNPipe(0, 1, True)
                for i in range(NT):
                    def fo(e, i=i):
                        for half in range(2):
                            for cc in range(8):
                                ins = e.matmul(PSD[:, half * 512:(half + 1) * 512], lhsT=OT[:, cc, i * 128:(i + 1) * 128],
                                               rhs=Wo[:, cc, half * 512:(half + 1) * 512], start=(cc == 0), stop=(cc == 7))
                        return ins
                    p.op("pe", fo, reads=["OT%d_%d" % (hp, h) for hp in range(8) for h in range(2)] + ["Wo"], writes=kD)
                    wpipe.push(i, PSD[:, :], kD)
                wpipe.flush()
                p.phase_end()

        def run_unit(s, c):
            load_unit(s, c)
            for l in range(DEPTH):
                if l < 2:
                    pool_phase(l, c, True)
                else:
                    attn_phase(l - 2, c)
                if stop == "ln1_%d" % l:
                    return
                load_ln2(l)
                load_ln1((l + 1) % DEPTH)
                ffn_phase(l, c, l in (1, 2))
                if stop == "ln2_%d" % l:
                    return
                if l == 1:
                    kv_phase(c)
                    if stop == "kv":
                        return

        load_ln1(0)
        for s in range(nseq):
            for c in range(2):
                run_unit(s, c)
                store_unit(s, c)
                p.phase_end()
        p.finish()
        print("instructions emitted:", p.ninstr, "semaphores:", p.nsem)
    return nc


def host_constants():
    c = {}
    c["c_ident"] = np.eye(128, dtype=np.float32)
    pb = np.zeros((128, 12, 128), np.float32)
    s_ = np.arange(128)[:, None]
    t_ = np.arange(128)[None, :]
    for wi, w in enumerate(POOL_W):
        band = ((t_ - s_) >= 0) & ((t_ - s_) < w)
        eye = (s_ == t_).astype(np.float32)
        pb[:, wi * 3 + 0, :] = band.astype(np.float32) / w - eye
        pb[:, wi * 3 + 1, :] = ((t_ - s_ + 128) < w).astype(np.float32) / w
        cnt = np.minimum(t_ + 1, w).astype(np.float32)
        pb[:, wi * 3 + 2, :] = band.astype(np.float32) / cnt - eye
    c["c_poolB"] = pb
    cur = (s_ <= t_).astype(np.float32)
    prev = (s_ >= t_).astype(np.float32)
    m = np.zeros((128, 3, 512), np.float32)
    m[:, 0, :] = np.concatenate([cur, prev, cur, prev], axis=1)
    m[:, 1, :] = np.tile(cur[:, 0:64], (1, 8))
    m[:, 2, :] = np.tile(cur[:, 64:128], (1, 8))
    c["c_masks"] = m
    pidx = np.arange(128)
    km = np.zeros((128, 2), np.float32)
    for h in range(2):
        km[:, h] = (((pidx // 32) % 2) == h).astype(np.float32)
    c["c_km"] = km
    inv_freq = (np.float32(10000.0) ** (-(np.arange(0, 64, 2, dtype=np.float32)) / np.float32(64))).astype(np.float32)
    ang = (np.arange(S, dtype=np.float32)[:, None] * inv_freq[None, :]).astype(np.float32)
    cosv = np.cos(ang.astype(np.float64)).astype(np.float32).T
    sinv = np.sin(ang.astype(np.float64)).astype(np.float32).T
    i_of_p = pidx % 32
    a_of_p = pidx // 64
    c["c_cos"] = np.ascontiguousarray(cosv[i_of_p, :])
    sgn = np.where(a_of_p == 0, 1.0, -1.0).astype(np.float32)[:, None]
    c["c_sin"] = np.ascontiguousarray(sinv[i_of_p, :] * sgn)
    return c


def host_layouts(inp):
    o = {}
    f = lambda a: np.ascontiguousarray(a, dtype=np.float32)
    o["pw_l"] = f(inp["pool_w"].reshape(2, 4, 2, 128, 256).transpose(0, 3, 1, 2, 4))
    o["pool_scale"] = f(inp["pool_scale"])
    wq = inp["w_q"].reshape(2, 8, 128, 3, 8, 2, 2, 32)
    o["wq_l"] = f(wq.transpose(0, 3, 4, 2, 1, 6, 5, 7).reshape(2, 3, 8, 128, 8, 128))
    wk = inp["w_kv"][:, :3072].reshape(8, 128, 3, 8, 2, 2, 32)
    o["wk_l"] = f(wk.transpose(2, 3, 1, 0, 5, 4, 6).reshape(3, 8, 128, 8, 128))
    wv = inp["w_kv"][:, 3072:].reshape(8, 128, 3, 1024)
    o["wv_l"] = f(wv.transpose(2, 1, 0, 3))
    o["wo_l"] = f(inp["w_o"].reshape(2, 8, 128, 1024).transpose(0, 2, 1, 3))
    o["wg_l"] = f(inp["ffn_w_gate"].reshape(4, 8, 128, NJ, 128).transpose(0, 3, 2, 1, 4))
    o["wu_l"] = f(inp["ffn_w_up"].reshape(4, 8, 128, NJ, 128).transpose(0, 3, 2, 1, 4))
    o["wd_l"] = f(inp["ffn_w_down"].reshape(4, NJ, 128, 1024).transpose(0, 2, 1, 3))
    o["cw_l"] = f(inp["ffn_conv_w"].reshape(4, 3, NJ, 128).transpose(0, 3, 2, 1))
    o["cb_l"] = f(inp["ffn_conv_b"].reshape(4, NJ, 128).transpose(0, 2, 1))
    for n in ("ln1_g", "ln1_b", "ln2_g", "ln2_b"):
        o[n] = f(inp[n])
    return o


_CACHE = {}


def kernel(**inputs):
    x = np.ascontiguousarray(inputs["x"], dtype=np.float32)
    B = x.shape[0]
    per = B // N_CORES
    shared = host_layouts(inputs)
    shared.update(host_constants())
    if "nc" not in _CACHE:
        _CACHE["nc"] = build(nseq=per)
    nc = _CACHE["nc"]
    in_maps = []
    for ci in range(N_CORES):
        m = dict(shared)
        m["x"] = x[ci * per:(ci + 1) * per]
        in_maps.append(m)
    res = run_bass_kernel_spmd(nc, in_maps, core_ids=list(range(N_CORES)))
    out = np.concatenate([np.asarray(r["out"]) for r in res.results], axis=0)
    return out.astype(np.float32)
```

```python
import contextlib
import math
import numpy as np
import concourse.bass as bass
import concourse.mybir as mybir
from concourse.bass_utils import run_bass_kernel_spmd

F32 = mybir.dt.float32
BF16 = mybir.dt.bfloat16
AF = mybir.ActivationFunctionType
ALU = mybir.AluOpType

D = 1024
S = 2048
UT = 1024
NT = UT // 128
TC = 512
DFF = 2816
NJ = DFF // 128
DEPTH = 4
ALPHA = (2.0 * DEPTH) ** 0.25
EPS = 1e-5
POOL_W = (2, 4, 8, 16)
GROUPS = ((128, 1), (512, 4), (2048, 16))
SQC = math.sqrt(0.044715)
GELU_S = 2.0 * math.sqrt(2.0 / math.pi)
N_CORES = 8

ENGS = ("pe", "act", "dve", "pool", "sp")
SEM_ROT = 30000


class Prog:
    def __init__(self, nc, st):
        self.nc = nc
        self.st = st
        self.E = {"pe": nc.tensor, "act": nc.scalar, "dve": nc.vector, "pool": nc.gpsimd, "sp": nc.sync}
        self.nsem = 0
        self.esem = {}
        self.ecnt = {}
        for e in ENGS:
            self._new_esem(e)
        self.dsem = {}
        self.dcnt = {}
        self.waited = {e: {} for e in ENGS}
        self.last_w = {}
        self.readers = {}
        self.last_tok = {e: None for e in ENGS}
        self.dma_toks = {}
        self.pending = {e: [] for e in ENGS}
        self.ninstr = 0
        self.alias_toks = []
        self.seen = None

    def _new_sem(self, name):
        s = self.st.enter_context(self.nc.semaphore(name))
        self.nsem += 1
        return s

    def _new_esem(self, e):
        self.esem[e] = (self._new_sem("e_%s_%d" % (e, self.nsem)), self.nsem)
        self.ecnt[e] = 0

    def _wait(self, e, toks):
        w = self.waited[e]
        eng = self.E[e]
        for t in toks:
            sem, sid, cnt, _ = t
            if w.get(sid, 0) >= cnt:
                continue
            eng.wait_ge(sem, cnt)
            w[sid] = cnt
            self.ninstr += 1

    def _deps(self, eq, ecmp, reads, writes):
        toks = []
        lw = self.last_w
        rd = self.readers
        for k in reads:
            t = lw.get(k)
            if t is not None and not (ecmp == "pe" and t[3] == "pe"):
                toks.append(t)
        same_ok = ecmp in ("pe", "act", "dve")
        for k in writes:
            t = lw.get(k)
            if t is not None and (not same_ok or t[3] != ecmp):
                toks.append(t)
            for t in rd.get(k, ()):
                if not same_ok or t[3] != ecmp:
                    toks.append(t)
        if self.pending[eq]:
            toks.extend(self.pending[eq])
            self.pending[eq] = []
        if self.seen is not None:
            fresh = False
            for k in reads:
                if k not in self.seen:
                    self.seen.add(k)
                    fresh = fresh or not self._persistent(k)
            for k in writes:
                if k not in self.seen:
                    self.seen.add(k)
                    fresh = fresh or not self._persistent(k)
            if fresh:
                toks.extend(self.alias_toks)
        return toks

    def _record(self, tok, reads, writes):
        rd = self.readers
        for k in reads:
            l = rd.get(k)
            if l is None:
                rd[k] = [tok]
            else:
                l.append(tok)
        for k in writes:
            self.last_w[k] = tok
            rd[k] = []

    def op(self, e, fn, reads=(), writes=()):
        self._wait(e, self._deps(e, e, reads, writes))
        if self.ecnt[e] >= SEM_ROT:
            self._new_esem(e)
        ins = fn(self.E[e])
        sem, sid = self.esem[e]
        self.ecnt[e] += 1
        ins.then_inc(sem, 1)
        tok = (sem, sid, self.ecnt[e], e)
        self.last_tok[e] = tok
        self._record(tok, reads, writes)
        self.ninstr += 1
        return tok

    def dma(self, e, fn, key, n=1, reads=(), writes=()):
        self._wait(e, self._deps(e, None, reads, writes))
        if key not in self.dsem:
            self.dsem[key] = (self._new_sem("d_%s" % key), self.nsem)
            self.dcnt[key] = 0
        sem, sid = self.dsem[key]
        fn(self.E[e], sem)
        self.dcnt[key] += 16 * n
        tok = (sem, sid, self.dcnt[key], None)
        self.dma_toks[sid] = tok
        self._record(tok, reads, writes)
        self.ninstr += n
        return tok

    def barrier(self):
        toks = [t for t in self.last_tok.values() if t is not None] + list(self.dma_toks.values())
        for e in ENGS:
            self.pending[e] = list(toks)
        self.last_w = {}
        self.readers = {}
        self.dma_toks = {}
        self.alias_toks = []
        self.seen = None

    PERSIST = ("X", "halo", "cy", "ident", "poolB", "masks", "km", "eps", "lnp", "sta", "stb", "mv", "sd",
               "rs", "nm", "A0", "A1", "B0", "B1", "C0", "C1", "D0", "D1")

    @staticmethod
    def _persistent(k):
        if k in ("A0", "A1", "B0", "B1", "C0", "C1", "D0", "D1", "ident", "poolB", "masks", "km", "eps", "pws", "cw"):
            return True
        for pre in ("X", "halo", "cy", "lnp", "sta", "stb", "mv", "sd", "rs", "nm"):
            if k.startswith(pre) and (len(k) == len(pre) or k[len(pre)].isdigit() or k[len(pre)] in "bT"):
                return True
        return False

    def phase_end(self):
        best = {}
        for t in getattr(self, "alias_toks", []) or []:
            if t[1] not in best or best[t[1]][2] < t[2]:
                best[t[1]] = t
        dead = []
        for k, t in self.last_w.items():
            if not self._persistent(k):
                dead.append(k)
                if t[1] not in best or best[t[1]][2] < t[2]:
                    best[t[1]] = t
        for k, l in self.readers.items():
            if not self._persistent(k):
                if k not in self.last_w:
                    dead.append(k)
                for t in l:
                    if t[1] not in best or best[t[1]][2] < t[2]:
                        best[t[1]] = t
        for k in dead:
            self.last_w.pop(k, None)
            self.readers.pop(k, None)
        self.alias_toks = list(best.values())
        self.seen = set(self.last_w.keys()) | set(self.readers.keys())

    def finish(self):
        toks = [t for t in self.last_tok.values() if t is not None] + list(self.dma_toks.values())
        self._wait("sp", toks)


def build(nseq=4, stop=None, dump_kv=False):
    nc = bass.Bass("TRN2", target_bir_lowering=False)

    def din(name, shape):
        return nc.dram_tensor(name, list(shape), F32, kind="ExternalInput").ap()

    x_d = din("x", [nseq, S, D])
    pw_d = din("pw_l", [2, 128, 4, 2, 256])
    psc_d = din("pool_scale", [2, D])
    wq_d = din("wq_l", [2, 3, 8, 128, 8, 128])
    wk_d = din("wk_l", [3, 8, 128, 8, 128])
    wv_d = din("wv_l", [3, 128, 8, 1024])
    wo_d = din("wo_l", [2, 128, 8, 1024])
    wg_d = din("wg_l", [4, NJ, 128, 8, 128])
    wu_d = din("wu_l", [4, NJ, 128, 8, 128])
    wd_d = din("wd_l", [4, 128, NJ, 1024])
    cw_d = din("cw_l", [4, 128, NJ, 3])
    cb_d = din("cb_l", [4, 128, NJ])
    ln_d = [din(n, [4, D]) for n in ("ln1_g", "ln1_b", "ln2_g", "ln2_b")]
    ident_d = din("c_ident", [128, 128])
    poolB_d = din("c_poolB", [128, 12, 128])
    masks_d = din("c_masks", [128, 3, 512])
    km_d = din("c_km", [128, 2])
    cos_d = din("c_cos", [128, S])
    sin_d = din("c_sin", [128, S])
    out_d = nc.dram_tensor("out", [nseq, S, D], F32, kind="ExternalOutput").ap()
    kvkind = "ExternalOutput" if dump_kv else "Internal"
    kt_hbm = nc.dram_tensor("kt_hbm", [3, 8, 2, 128, S], BF16, kind=kvkind).ap()
    v_hbm = nc.dram_tensor("v_hbm", [S, 3, 16, 128], BF16, kind=kvkind).ap()

    uid = [0]

    with contextlib.ExitStack() as st:
        p = Prog(nc, st)

        def sbuf(stack, name, shape, dt=F32):
            uid[0] += 1
            return stack.enter_context(nc.sbuf_tensor("%s_%d" % (name, uid[0]), list(shape), dt))

        X = sbuf(st, "X", [128, NT, D])
        Xb = sbuf(st, "Xb", [128, NT, D], BF16)
        XT = sbuf(st, "XT", [128, 8, UT], BF16)
        halo = sbuf(st, "halo", [128, 2, D], BF16)
        carry = sbuf(st, "carry", [128, 4, NJ, 2])
        ident = sbuf(st, "ident", [128, 128], BF16)
        poolB = sbuf(st, "poolB", [128, 12, 128], BF16)
        masks = sbuf(st, "masks", [128, 3, 512], BF16)
        km = sbuf(st, "km", [128, 2])
        epsT = sbuf(st, "epsT", [128, 1])
        lnp = sbuf(st, "lnp", [128, 4, D])
        stats = sbuf(st, "stats", [128, 4, 2, 6])
        mv = sbuf(st, "mv", [128, 4, 2])
        sd = sbuf(st, "sd", [128, 4, 4])
        pwsP = sbuf(st, "pwsP", [128, 2, 4, 2, 256], BF16)
        cwP = sbuf(st, "cwP", [128, 4, NJ, 3])
        cbP = sbuf(st, "cbP", [128, 4, NJ])
        PSA = st.enter_context(nc.psum_tensor("PSA", [128, 1024], F32))
        PSB = st.enter_context(nc.psum_tensor("PSB", [128, 1024], F32))
        PSD = st.enter_context(nc.psum_tensor("PSD", [128, 1024], F32))
        PSC = [st.enter_context(nc.psum_tensor("PSC%d" % i, [128, 512], F32)) for i in range(2)]
        TRB = [PSC[i][:, :].bitcast(BF16).rearrange("p (k n) -> p k n", n=128) for i in range(2)]
        kA = ["A0", "A1"]
        kB = ["B0", "B1"]
        kD = ["D0", "D1"]
        kC = ["C0", "C1"]

        def ld_consts(e, s):
            e.dma_start(out=ident[:], in_=ident_d[:, :]).then_inc(s, 16)
            e.dma_start(out=poolB[:], in_=poolB_d[:, :, :]).then_inc(s, 16)
            e.dma_start(out=masks[:], in_=masks_d[:, :, :]).then_inc(s, 16)
        p.dma("pool", ld_consts, "const", n=3, writes=["ident", "poolB", "masks"])
        p.dma("sp", lambda e, s: e.dma_start(out=km[:], in_=km_d[:, :]).then_inc(s, 16), "const2", writes=["km"])
        p.op("dve", lambda e: e.memset(epsT[:], EPS), writes=["eps"])

        par_state = {"ln": 0, "xt": 0}

        with contextlib.ExitStack() as ph0:
            pwf0 = sbuf(ph0, "pwf0", [128, 2, 4, 2, 256], BF16)
            psc0 = sbuf(ph0, "psc0", [128, 2, D])

            def ld0(e, s):
                for l_ in range(2):
                    e.dma_start(out=pwf0[:, l_, :, :, :], in_=pw_d[l_]).then_inc(s, 16)
            p.dma("pool", ld0, "pwf", n=2, writes=["pwf"])

            def ld1(e, s):
                for l_ in range(2):
                    e.dma_start(out=psc0[:, l_, :], in_=psc_d[l_, :].partition_broadcast(128)).then_inc(s, 16)
                for l_ in range(4):
                    e.dma_start(out=cwP[:, l_, :, :], in_=cw_d[l_]).then_inc(s, 16)
                    e.dma_start(out=cbP[:, l_, :], in_=cb_d[l_]).then_inc(s, 16)
            p.dma("sp", ld1, "psc", n=10, writes=["psc", "cw"])
            for l_ in range(2):
                for g in range(4):
                    for kc in range(2):
                        p.op("pool", (lambda l_, g, kc: lambda e: e.tensor_tensor(
                            pwsP[:, l_, g, kc, :], pwf0[:, l_, g, kc, :], psc0[:, l_, g * 256:(g + 1) * 256], ALU.mult))(l_, g, kc),
                            reads=["pwf", "psc"], writes=["pws"])
            p.phase_end()

        ln_ctr = [0]

        class LNPipe:
            NST = 5

            def __init__(self, gi, bi, want_xt):
                self.items = []
                self.gi, self.bi, self.want_xt = gi, bi, want_xt

            def _stage(self, k, item):
                i, ps_ap, pskeys, gidx = item
                par = gidx % 4
                gi, bi = self.gi, self.bi
                Xi = X[:, i, :]
                kx = "X%d" % i
                sa, sb_, smv, ssd, srs, snm = ("sta%d" % par, "stb%d" % par, "mv%d" % par, "sd%d" % par,
                                               "rs%d" % par, "nm%d" % par)
                if k == 0:
                    p.op("dve", lambda e: e.scalar_tensor_tensor(Xi, Xi, ALPHA, ps_ap, ALU.mult, ALU.add),
                         reads=list(pskeys) + [kx], writes=[kx])
                    p.op("dve", lambda e: e.bn_stats(stats[:, par, 0, :], X[:, i, 0:512]), reads=[kx], writes=[sa])
                    p.op("dve", lambda e: e.bn_stats(stats[:, par, 1, :], X[:, i, 512:1024]), reads=[kx], writes=[sb_])
                    p.op("dve", lambda e: e.bn_aggr(mv[:, par, :], stats[:, par, :, :]), reads=[sa, sb_], writes=[smv])
                elif k == 1:
                    p.op("act", lambda e: e.activation(sd[:, par, 0:1], mv[:, par, 1:2], AF.Sqrt, bias=epsT[:], scale=1.0),
                         reads=[smv, "eps"], writes=[ssd])
                    p.op("dve", lambda e: e.reciprocal(sd[:, par, 1:2], sd[:, par, 0:1]), reads=[ssd], writes=[srs])
                    p.op("dve", lambda e: e.tensor_scalar(sd[:, par, 2:3], mv[:, par, 0:1], sd[:, par, 1:2], -1.0,
                                                          ALU.mult, ALU.mult), reads=[smv, srs], writes=[snm])
                elif k == 2:
                    p.op("act", lambda e: e.activation(Xi, Xi, AF.Identity, bias=sd[:, par, 2:3], scale=sd[:, par, 1:2]),
                         reads=[kx, srs, snm], writes=[kx])
                    p.op("dve", lambda e: e.tensor_tensor(Xi, Xi, lnp[:, gi, :], ALU.mult), reads=[kx, "lnp%d" % gi], writes=[kx])
                    p.op("dve", lambda e: e.tensor_tensor(Xi, Xi, lnp[:, bi, :], ALU.add), reads=[kx, "lnp%d" % bi], writes=[kx])
                elif k == 3:
                    p.op("act", lambda e: e.activation(Xb[:, i, :], Xi, AF.Identity), reads=[kx], writes=["Xb%d" % i])
                elif k == 4 and self.want_xt:
                    xt_tile(i)

            def _step(self, s):
                n = len(self.items)
                for k in range(self.NST):
                    j = s - k
                    if 0 <= j < n:
                        self._stage(k, self.items[j])

            def push(self, i, ps_ap, pskeys):
                self.items.append((i, ps_ap, pskeys, ln_ctr[0]))
                ln_ctr[0] += 1
                self._step(len(self.items) - 1)

            def flush(self):
                n = len(self.items)
                for s in range(n, n + self.NST - 1):
                    self._step(s)
                self.items = []

        def xt_tile(i):
            par = par_state["xt"]
            par_state["xt"] ^= 1
            trp = TRB[par]

            def f(e):
                for k in range(8):
                    ins = e.transpose(trp[:, k, :], Xb[:, i, k * 128:(k + 1) * 128], ident[:])
                return ins
            p.op("pe", f, reads=["Xb%d" % i, "ident"], writes=[kC[par]])
            p.op("act", lambda e: e.activation(XT[:, :, i * 128:(i + 1) * 128], trp, AF.Identity),
                 reads=[kC[par]], writes=["XT%d" % i])

        def load_ln1(l):
            def f(e, s):
                for q in range(2):
                    e.dma_start(out=lnp[:, q, :], in_=ln_d[q][l, :].partition_broadcast(128)).then_inc(s, 16)
            p.dma("sp", f, "lnpA", n=2, writes=["lnp0", "lnp1"])

        def load_ln2(l):
            def f(e, s):
                for q in range(2, 4):
                    e.dma_start(out=lnp[:, q, :], in_=ln_d[q][l, :].partition_broadcast(128)).then_inc(s, 16)
            p.dma("sp", f, "lnpB", n=2, writes=["lnp2", "lnp3"])

        def load_unit(s, c):
            for i in range(NT):
                r0 = c * UT + i * 128
                p.dma("sp", (lambda i, r0: lambda e, sm: e.dma_start(out=X[:, i, :], in_=x_d[s, r0:r0 + 128, :]).then_inc(sm, 16))(i, r0),
                      "ldx%d" % i, writes=["X%d" % i])
            for i in range(NT):
                if i % 2 == 0:
                    p.op("act", (lambda i: lambda e: e.activation(Xb[:, i, :], X[:, i, :], AF.Identity))(i),
                         reads=["X%d" % i], writes=["Xb%d" % i])
                else:
                    p.op("dve", (lambda i: lambda e: e.tensor_copy(Xb[:, i, :], X[:, i, :]))(i),
                         reads=["X%d" % i], writes=["Xb%d" % i])
            if c == 0:
                p.op("pool", lambda e: e.memset(carry[:], 0.0),
                     writes=["cy%d_%d" % (l, j) for l in range(4) for j in range(NJ)])

        def store_unit(s, c):
            for i in range(NT):
                r0 = c * UT + i * 128
                p.dma("sp", (lambda i, r0: lambda e, sm: e.dma_start(out=out_d[s, r0:r0 + 128, :], in_=X[:, i, :]).then_inc(sm, 16))(i, r0),
                      "stx%d" % i, reads=["X%d" % i])

        def pool_phase(l, c, want_xt):
            with contextlib.ExitStack() as ph:
                pmT = sbuf(ph, "pmT", [128, 2, 8, 128], BF16)
                if c == 0:
                    p.op("act", lambda e: e.activation(halo[:, l, :], Xb[:, NT - 1, :], AF.Identity),
                         reads=["Xb%d" % (NT - 1)], writes=["halo%d" % l])
                pwskeys = ["pws"]

                def pm(i):
                    ps = PSA if i % 2 == 0 else PSB
                    keys = kA if i % 2 == 0 else kB
                    first = (c == 0 and i == 0)
                    hasprev = not first
                    rk = ["Xb%d" % i, "poolB"]
                    if hasprev:
                        rk.append("Xb%d" % (i - 1) if i > 0 else "halo%d" % l)

                    def f(e):
                        for cc in range(8):
                            g = cc // 2
                            bsel = poolB[:, g * 3 + (2 if first else 0), :]
                            ins = e.matmul(ps[:, cc * 128:(cc + 1) * 128], lhsT=Xb[:, i, cc * 128:(cc + 1) * 128],
                                           rhs=bsel, start=True, stop=not hasprev)
                            if hasprev:
                                src = Xb[:, i - 1, cc * 128:(cc + 1) * 128] if i > 0 else halo[:, l, cc * 128:(cc + 1) * 128]
                                ins = e.matmul(ps[:, cc * 128:(cc + 1) * 128], lhsT=src, rhs=poolB[:, g * 3 + 1, :],
                                               start=False, stop=True)
                        return ins
                    p.op("pe", f, reads=rk, writes=keys)
                    p.op("act", lambda e: e.activation(pmT[:, i % 2, :, :], ps[:, :].rearrange("p (k n) -> p k n", n=128),
                                                       AF.Identity), reads=keys, writes=["pmT%d" % (i % 2)])

                def rest(i):
                    def f(e):
                        for g in range(4):
                            for kc in range(2):
                                ins = e.matmul(PSD[:, g * 256:(g + 1) * 256], lhsT=pmT[:, i % 2, 2 * g + kc, :],
                                               rhs=pwsP[:, l, g, kc, :], start=(kc == 0), stop=(kc == 1))
                        return ins
                    p.op("pe", f, reads=["pmT%d" % (i % 2)] + pwskeys, writes=kD)

                pipe = LNPipe(0, 1, want_xt)
                pm(0)
                for i in range(NT):
                    if i + 1 < NT:
                        pm(i + 1)
                    rest(i)
                    pipe.push(i, PSD[:, :], kD)
                pipe.flush()
                p.phase_end()

        def ffn_phase(l, c, want_xt):
            with contextlib.ExitStack() as ph:
                hT = sbuf(ph, "hT", [128, NJ, TC], BF16)
                Wd = sbuf(ph, "Wd", [128, NJ, D], BF16)
                wg = [sbuf(ph, "wg%d" % i, [128, 8, 128], BF16) for i in range(4)]
                wu = [sbuf(ph, "wu%d" % i, [128, 8, 128], BF16) for i in range(4)]
                gbuf = sbuf(ph, "gbuf", [128, 2, TC + 2])
                cbf = sbuf(ph, "cbf", [128, 3, TC])
                sq = sbuf(ph, "sq", [128, 2, TC])
                sg = sbuf(ph, "sg", [128, 2, TC])
                cu = sbuf(ph, "cu", [128, 2, TC])
                cw = cwP[:, l, :, :]
                cbias = cbP[:, l, :]

                def ld_w(q):
                    slot = q % 4
                    j = q % NJ

                    def f(e, s):
                        e.dma_start(out=wg[slot][:], in_=wg_d[l, j]).then_inc(s, 16)
                        e.dma_start(out=wu[slot][:], in_=wu_d[l, j]).then_inc(s, 16)
                    p.dma("pool", f, "wgu%d" % slot, n=2, writes=["wgu%d" % slot])

                ld_w(0)
                ld_w(1)
                bounds = [0, 6, 12, 17, NJ]

                def ldwd(e, s):
                    for q in range(4):
                        a, b = bounds[q], bounds[q + 1]
                        e.dma_start(out=Wd[:, a:b, :], in_=wd_d[l][:, a:b, :]).then_inc(s, 16)
                NQ = 2 * NJ

                def st_pe(q):
                    ch, j = divmod(q, NJ)
                    par = q % 2
                    slot = q % 4
                    gp = PSA[:, par * 512:(par + 1) * 512]
                    up = PSB[:, par * 512:(par + 1) * 512]
                    xtk = ["XT%d" % (ch * 4 + t) for t in range(4)]

                    def fg(e):
                        for k in range(8):
                            ins = e.matmul(gp, lhsT=wg[slot][:, k, :], rhs=XT[:, k, ch * TC:(ch + 1) * TC],
                                           start=(k == 0), stop=(k == 7))
                        return ins

                    def fu(e):
                        for k in range(8):
                            ins = e.matmul(up, lhsT=wu[slot][:, k, :], rhs=XT[:, k, ch * TC:(ch + 1) * TC],
                                           start=(k == 0), stop=(k == 7))
                        return ins
                    p.op("pe", fg, reads=["wgu%d" % slot] + xtk, writes=[kA[par]])
                    p.op("pe", fu, reads=["wgu%d" % slot] + xtk, writes=[kB[par]])

                def st_s1(q):
                    ch, j = divmod(q, NJ)
                    par = q % 2
                    c3 = q % 3
                    gp = PSA[:, par * 512:(par + 1) * 512]
                    gk = kA[par]
                    cbp = cbf[:, c3, :]
                    kcy = "cy%d_%d" % (l, j)
                    kgh, kgb, kc_ = "gbh%d" % par, "gbb%d" % par, "cb%d" % c3
                    p.op("pool", lambda e: e.tensor_copy(gbuf[:, par, 0:2], carry[:, l, j, :]), reads=[kcy], writes=[kgh])
                    p.op("act", lambda e: e.activation(gbuf[:, par, 2:TC + 2], gp, AF.Identity), reads=[gk], writes=[kgb])
                    p.op("act", lambda e: e.activation(cbp, gp, AF.Identity, bias=cbias[:, j:j + 1], scale=cw[:, j, 2:3]),
                         reads=[gk, "cw"], writes=[kc_])
                    p.op("pool", lambda e: e.tensor_copy(carry[:, l, j, :], gbuf[:, par, TC:TC + 2]), reads=[kgb], writes=[kcy])
                    p.op("dve", lambda e: e.scalar_tensor_tensor(cbp, gbuf[:, par, 1:TC + 1], cw[:, j, 1:2], cbp, ALU.mult, ALU.add),
                         reads=[kgb, kgh, kc_, "cw"], writes=[kc_])
                    p.op("dve", lambda e: e.scalar_tensor_tensor(cbp, gbuf[:, par, 0:TC], cw[:, j, 0:1], cbp, ALU.mult, ALU.add),
                         reads=[kgb, kgh, kc_, "cw"], writes=[kc_])

                def st_s2(q):
                    par = q % 2
                    c3 = q % 3
                    cbp = cbf[:, c3, :]
                    sqp = sq[:, par, :]
                    kc_, ksq = "cb%d" % c3, "sq%d" % par
                    p.op("act", lambda e: e.activation(sqp, cbp, AF.Square, scale=SQC), reads=[kc_], writes=[ksq])
                    p.op("dve", lambda e: e.scalar_tensor_tensor(sqp, sqp, 1.0, cbp, ALU.add, ALU.mult), reads=[ksq, kc_], writes=[ksq])

                def st_d4(q):
                    par = q % 2
                    c3 = q % 3
                    up = PSB[:, par * 512:(par + 1) * 512]
                    p.op("dve", lambda e: e.tensor_tensor(cu[:, par, :], cbf[:, c3, :], up, ALU.mult),
                         reads=["cb%d" % c3, kB[par]], writes=["cu%d" % par])

                def st_s3(q):
                    ch, j = divmod(q, NJ)
                    par = q % 2
                    p.op("act", lambda e: e.activation(sg[:, par, :], sq[:, par, :], AF.Sigmoid, scale=GELU_S),
                         reads=["sq%d" % par], writes=["sg%d" % par])
                    p.op("dve", lambda e: e.tensor_tensor(hT[:, j, :], cu[:, par, :], sg[:, par, :], ALU.mult),
                         reads=["cu%d" % par, "sg%d" % par], writes=["hT%d" % j])

                def step(t):
                    if 0 <= t - 2 < NQ:
                        st_d4(t - 2)
                    if t < NQ:
                        if t + 2 < NQ:
                            ld_w(t + 2)
                        if t == 3:
                            p.dma("pool", ldwd, "Wd", n=4, writes=["Wd"])
                        st_pe(t)
                        st_s1(t)
                    if 0 <= t - 1 < NQ:
                        st_s2(t - 1)
                    if 0 <= t - 2 < NQ:
                        st_s3(t - 2)

                def down(ch, t):
                    i = ch * 4 + t

                    def f(e):
                        for half in range(2):
                            for j in range(NJ):
                                ins = e.matmul(PSD[:, half * 512:(half + 1) * 512], lhsT=hT[:, j, t * 128:(t + 1) * 128],
                                               rhs=Wd[:, j, half * 512:(half + 1) * 512], start=(j == 0), stop=(j == NJ - 1))
                        return ins
                    p.op("pe", f, reads=["hT%d" % j for j in range(NJ)] + ["Wd"], writes=kD)

                def downs(ch):
                    pipe = LNPipe(2, 3, want_xt)
                    for t in range(4):
                        down(ch, t)
                        pipe.push(ch * 4 + t, PSD[:, :], kD)
                    pipe.flush()

                for t in range(NQ + 2):
                    step(t)
                    if t == NJ + 1:
                        downs(0)
                downs(1)
                p.phase_end()

        def rope(ps_ap, pskey, cosap, sinap, ra, ru, par, out_fn):
            kra, kru0, kru1 = "ra%d" % par, "ru0_%d" % par, "ru1_%d" % par
            p.op("dve", lambda e: e.tensor_tensor(ra[:, par, :], ps_ap, cosap, ALU.mult), reads=[pskey, "rope"], writes=[kra])
            p.op("dve", lambda e: e.tensor_tensor(ru[64:128, par, :], ps_ap[0:64, :], sinap[0:64, :], ALU.mult),
                 reads=[pskey, "rope"], writes=[kru1])
            p.op("dve", lambda e: e.tensor_tensor(ru[0:64, par, :], ps_ap[64:128, :], sinap[64:128, :], ALU.mult),
                 reads=[pskey, "rope"], writes=[kru0])
            out_fn(ra[:, par, :], ru[:, par, :], [kra, kru0, kru1])

        def kv_phase(c):
            with contextlib.ExitStack() as ph:
                cosU = sbuf(ph, "cosU", [128, UT])
                sinU = sbuf(ph, "sinU", [128, UT])
                wvb = [sbuf(ph, "wv%d" % i, [128, 8, D], BF16) for i in range(2)]
                wk = [sbuf(ph, "wk%d" % i, [128, 8, 128], BF16) for i in range(4)]
                kst = [sbuf(ph, "kst%d" % i, [128, 2, UT], BF16) for i in range(2)]
                vst = [sbuf(ph, "vst%d" % i, [128, 16, 128], BF16) for i in range(2)]
                ra = sbuf(ph, "ra", [128, 2, TC])
                ru = sbuf(ph, "ru", [128, 2, TC])
                rs = sbuf(ph, "rs", [128, 2, TC])

                def ldr(e, s):
                    e.dma_start(out=cosU[:], in_=cos_d[:, c * UT:(c + 1) * UT]).then_inc(s, 16)
                    e.dma_start(out=sinU[:], in_=sin_d[:, c * UT:(c + 1) * UT]).then_inc(s, 16)
                p.dma("sp", ldr, "rope", n=2, writes=["rope"])
                for i in range(2):
                    p.op("pool", (lambda i: lambda e: e.memset(vst[i][:], 1.0))(i), writes=["vst%d" % i, "vst%db" % i])
                xtall = ["XT%d" % i for i in range(NT)]
                cnt = 0
                vstate = [0]
                def ld_wk(n):
                    g_, hp_ = divmod(n, 8)
                    sl_ = n % 4
                    p.dma("pool", lambda e, s: e.dma_start(out=wk[sl_][:], in_=wk_d[g_, hp_]).then_inc(s, 16),
                          "wk%d" % sl_, writes=["wk%d" % sl_])

                def ld_wv(g_):
                    p.dma("pool", lambda e, s: e.dma_start(out=wvb[g_ % 2][:], in_=wv_d[g_]).then_inc(s, 16),
                          "wv%d" % (g_ % 2), writes=["wv%d" % (g_ % 2)])
                ld_wk(0)
                ld_wk(1)
                ld_wv(0)
                for g in range(3):
                    wv = wvb[g % 2]
                    wvk = "wv%d" % (g % 2)
                    if g + 1 < 3:
                        ld_wv(g + 1)
                    def v_tile(t, g=g, wv=wv, wvk=wvk):
                        vs = vstate[0] % 2
                        vstate[0] += 1
                        vp = PSB if vs == 0 else PSD
                        vk = kB if vs == 0 else kD

                        def fv(e, t=t, vp=vp, wv=wv):
                            for half in range(2):
                                for k in range(8):
                                    ins = e.matmul(vp[:, half * 512:(half + 1) * 512], lhsT=XT[:, k, t * 128:(t + 1) * 128],
                                                   rhs=wv[:, k, half * 512:(half + 1) * 512], start=(k == 0), stop=(k == 7))
                            return ins
                        p.op("pe", fv, reads=[wvk, "XT%d" % t], writes=vk)
                        vview = vp[:, :].rearrange("p (h c) -> p h c", c=64)
                        p.op("act", (lambda vs, vview: lambda e: e.activation(vst[vs][:, 0:16:2, 0:64], vview[:, 0:16:2, :], AF.Identity))(vs, vview),
                             reads=vk, writes=["vst%d" % vs])
                        p.op("act", (lambda vs, vview: lambda e: e.activation(vst[vs][:, 1:16:2, 64:128], vview[:, 1:16:2, :], AF.Identity))(vs, vview),
                             reads=vk, writes=["vst%db" % vs])
                        r0 = c * UT + t * 128
                        p.dma("sp", (lambda vs, g, r0: lambda e, s: e.dma_start(out=v_hbm[r0:r0 + 128, g, :, :], in_=vst[vs][:]).then_inc(s, 16))(vs, g, r0),
                              "vst%d" % vs, reads=["vst%d" % vs, "vst%db" % vs])

                    for hp in range(8):
                        sl = cnt % 4
                        ks = cnt % 2
                        if cnt + 2 < 24:
                            ld_wk(cnt + 2)
                        cnt += 1
                        for tc in range(2):
                            par = tc
                            kp = PSA[:, par * 512:(par + 1) * 512]

                            def fk(e, sl=sl, tc=tc, kp=kp):
                                for k in range(8):
                                    ins = e.matmul(kp, lhsT=wk[sl][:, k, :], rhs=XT[:, k, tc * TC:(tc + 1) * TC],
                                                   start=(k == 0), stop=(k == 7))
                                return ins
                            p.op("pe", fk, reads=["wk%d" % sl] + xtall[tc * 4:(tc + 1) * 4], writes=[kA[par]])

                            def fin(ra_ap, ru_ap, keys, sl=ks, tc=tc, par=par):
                                p.op("dve", lambda e: e.tensor_tensor(rs[:, par, :], ra_ap, ru_ap, ALU.add), reads=keys, writes=["krs%d" % par])
                                for h in range(2):
                                    p.op("act", (lambda h: lambda e: e.activation(kst[sl][:, h, tc * TC:(tc + 1) * TC], rs[:, par, :],
                                                                                  AF.Identity, scale=km[:, h:h + 1]))(h),
                                         reads=["krs%d" % par, "km"], writes=["kst%d_%d_%d" % (sl, h, tc)])
                            rope(kp, kA[par], cosU[:, tc * TC:(tc + 1) * TC], sinU[:, tc * TC:(tc + 1) * TC], ra, ru, par, fin)
                        dst = kt_hbm[g, hp, :, :, c * UT:(c + 1) * UT].rearrange("h q t -> q h t")
                        p.dma("sp", (lambda dst, sl: lambda e, s: e.dma_start(out=dst, in_=kst[sl][:]).then_inc(s, 16))(dst, ks),
                              "kst%d" % ks, reads=["kst%d_%d_%d" % (ks, h, tc) for h in range(2) for tc in range(2)])
                        v_tile(hp)
                p.phase_end()

        def attn_phase(jl, c):
            KL = UT * (c + 1)
            with contextlib.ExitStack() as ph:
                cosU = sbuf(ph, "cosU", [128, UT])
                sinU = sbuf(ph, "sinU", [128, UT])
                OT = sbuf(ph, "OT", [128, 8, UT], BF16)
                Wo = sbuf(ph, "Wo", [128, 8, D], BF16)
                wq = [sbuf(ph, "wq%d" % i, [128, 8, 128], BF16) for i in range(3)]
                QT = [sbuf(ph, "QT%d" % i, [128, UT], BF16) for i in range(2)]
                kt = [sbuf(ph, "kt%d" % i, [128, 2, S], BF16) for i in range(2)]
                vb = [sbuf(ph, "vb%d" % i, [128, 16, 256], BF16) for i in range(2)]
                esb = [sbuf(ph, "esb%d" % i, [128, 512], BF16) for i in range(2)]
                PT = [sbuf(ph, "PT%d" % i, [128, 512], BF16) for i in range(3)]
                rden = sbuf(ph, "rden", [128, UT])
                Osb = sbuf(ph, "Osb", [128, 2, UT])
                ra = sbuf(ph, "ra", [128, 2, TC])
                ru = sbuf(ph, "ru", [128, 2, TC])

                def ldr(e, s):
                    e.dma_start(out=cosU[:], in_=cos_d[:, c * UT:(c + 1) * UT]).then_inc(s, 16)
                    e.dma_start(out=sinU[:], in_=sin_d[:, c * UT:(c + 1) * UT]).then_inc(s, 16)
                p.dma("sp", ldr, "rope", n=2, writes=["rope"])
                p.dma("pool", lambda e, s: e.dma_start(out=Wo[:], in_=wo_d[jl]).then_inc(s, 16), "Wo", writes=["Wo"])
                xtall = ["XT%d" % i for i in range(NT)]
                its = [(hp, g) for hp in range(8) for g in range(3)]

                def ld_wq(it):
                    hp, g = its[it]
                    s3 = it % 3
                    p.dma("pool", lambda e, s: e.dma_start(out=wq[s3][:], in_=wq_d[jl, g, hp]).then_inc(s, 16), "wq%d" % s3, writes=["wq%d" % s3])

                def loads(it):
                    hp, g = its[it]
                    sl = it % 2
                    d = GROUPS[g][1]
                    NB = 16 // d
                    src = kt_hbm[g, hp, :, :, 0:KL].rearrange("h q t -> q h t")
                    p.dma("sp", lambda e, s: e.dma_start(out=kt[sl][:, :, 0:KL], in_=src).then_inc(s, 16), "kt%d" % sl, writes=["kt%d" % sl])
                    vsrc = v_hbm[:, g, 2 * hp:2 * hp + 2, :].rearrange("(n i r) h c -> i r n (h c)", n=NB, i=128, r=d)
                    vdst = vb[sl][:, :, :].rearrange("p (r n) x -> p r n x", r=d, n=NB)

                    def fv(e, s):
                        for r in range(d):
                            if d == 16:
                                rows = 64 * (c + 1)
                                e.dma_start(out=vdst[0:rows, r, 0, :], in_=vsrc[0:rows, r, 0, :]).then_inc(s, 16)
                            else:
                                nbv = NB // 2 * (c + 1)
                                e.dma_start(out=vdst[:, r, 0:nbv, :], in_=vsrc[:, r, 0:nbv, :]).then_inc(s, 16)
                    p.dma("sp", fv, "vb%d" % sl, n=d, writes=["vb%d" % sl])

                def qproj(it):
                    hp, g = its[it]
                    sl = it % 2
                    for tc in range(2):
                        par = tc
                        qp = PSA[:, par * 512:(par + 1) * 512]

                        def fq(e, tc=tc, qp=qp):
                            for k in range(8):
                                ins = e.matmul(qp, lhsT=wq[it % 3][:, k, :], rhs=XT[:, k, tc * TC:(tc + 1) * TC],
                                               start=(k == 0), stop=(k == 7))
                            return ins
                        p.op("pe", fq, reads=["wq%d" % (it % 3)] + xtall[tc * 4:(tc + 1) * 4], writes=[kA[par]])

                        def fin(ra_ap, ru_ap, keys, tc=tc):
                            p.op("dve", lambda e: e.tensor_tensor(QT[sl][:, tc * TC:(tc + 1) * TC], ra_ap, ru_ap, ALU.add),
                                 reads=keys, writes=["QT%d_%d" % (sl, tc)])
                        rope(qp, kA[par], cosU[:, tc * TC:(tc + 1) * TC], sinU[:, tc * TC:(tc + 1) * TC], ra, ru, par, fin)

                batches = []
                for it, (hp, g) in enumerate(its):
                    d = GROUPS[g][1]
                    NB = 16 // d
                    for h in range(2):
                        items = []
                        if d == 16:
                            kc = 64 * (c + 1)
                            for r in range(16):
                                items.append(dict(kc=kc, kstart=r, blk=r * NB, parts=[(0, r, 64)]))
                            groups_ = [items[0:8], items[8:16]]
                            mt = 1 + c
                            slotw = 64
                        else:
                            nblk = 1024 // (128 * d)
                            qlo, qhi = nblk * c, nblk * (c + 1)
                            for r in range(d):
                                for n in range(max(qlo - 1, 0), qhi):
                                    has_cur = qlo <= n < qhi
                                    has_prev = qlo <= n + 1 < qhi
                                    qc = 128 * n * d + r - c * UT
                                    qn = 128 * (n + 1) * d + r - c * UT
                                    if has_cur and has_prev:
                                        parts = [(0, qc, 256)]
                                    elif has_cur:
                                        parts = [(0, qc, 128)]
                                    else:
                                        parts = [(128, qn, 128)]
                                    items.append(dict(kc=128, kstart=128 * n * d + r, blk=r * NB + n, parts=parts))
                            groups_ = [items[i:i + 2] for i in range(0, len(items), 2)]
                            mt = 0
                            slotw = 256
                        for grp in groups_:
                            batches.append(dict(it=it, hp=hp, g=g, h=h, d=d, items=grp, mt=mt, slotw=slotw))

                def qk(bi):
                    b = batches[bi]
                    sl = b["it"] % 2
                    bp = bi % 2
                    Sp = PSC[bp]
                    d, h = b["d"], b["h"]

                    def f(e):
                        for ci, itx in enumerate(b["items"]):
                            kc = itx["kc"]
                            ks = itx["kstart"]
                            base = ci * b["slotw"]
                            for (so, q0, nq) in itx["parts"]:
                                ins = e.matmul(Sp[0:kc, base + so:base + so + nq],
                                               lhsT=kt[sl][:, h, ks:ks + (kc - 1) * d + 1:d],
                                               rhs=QT[sl][:, q0:q0 + (nq - 1) * d + 1:d], start=True, stop=True)
                        return ins
                    p.op("pe", f, reads=["kt%d" % sl, "QT%d_0" % sl, "QT%d_1" % sl], writes=[kC[bp]])
                    rows = b["items"][0]["kc"]
                    p.op("act", lambda e: e.activation(esb[bp][0:rows, :], Sp[0:rows, :], AF.Exp, scale=0.125),
                         reads=[kC[bp]], writes=["esb%d" % bp])
                    p3 = bi % 3
                    p.op("dve", lambda e: e.tensor_tensor(PT[p3][0:rows, :], esb[bp][0:rows, :], masks[0:rows, b["mt"], :], ALU.mult),
                         reads=["esb%d" % bp, "masks"], writes=["PT%d" % p3])

                def pv(bi, last):
                    b = batches[bi]
                    sl = b["it"] % 2
                    bp = bi % 3
                    d, h = b["d"], b["h"]
                    Op = PSB if h == 0 else PSD
                    ok = kB if h == 0 else kD

                    def f(e):
                        nmm = sum(len(itx["parts"]) for itx in b["items"])
                        m = 0
                        for ci, itx in enumerate(b["items"]):
                            kc = itx["kc"]
                            base = ci * b["slotw"]
                            for (so, q0, nq) in itx["parts"]:
                                m += 1
                                ins = e.matmul(Op[:, q0:q0 + (nq - 1) * d + 1:d],
                                               lhsT=vb[sl][0:kc, itx["blk"], h * 128:(h + 1) * 128],
                                               rhs=PT[bp][0:kc, base + so:base + so + nq],
                                               start=False, stop=(last and m == nmm), skip_group_check=True)
                        return ins
                    p.op("pe", f, reads=["vb%d" % sl, "PT%d" % bp], writes=ok)

                deferred = []

                def evac_head(hp, h, more):
                    Op = PSB if h == 0 else PSD
                    ok = kB if h == 0 else kD
                    p.op("act", lambda e: e.activation(Osb[:, h, :], Op[:, :], AF.Identity), reads=ok, writes=["Osb%d" % h])
                    if more:
                        p.op("dve", lambda e: e.memset(Op[:, :], 0.0), writes=ok)
                    if h == 0:
                        nsl, dsl = slice(0, 64), slice(64, 128)
                    else:
                        nsl, dsl = slice(64, 128), slice(0, 64)
                    kr = "rden%d" % h
                    deferred.append(lambda: p.op("act", lambda e: e.activation(rden[nsl, :], Osb[dsl, h, :], AF.Ln),
                                                 reads=["Osb%d" % h], writes=[kr]))
                    deferred.append(lambda: p.op("act", lambda e: e.activation(rden[nsl, :], rden[nsl, :], AF.Exp, scale=-1.0),
                                                 reads=[kr], writes=[kr]))
                    deferred.append(lambda: p.op("dve", lambda e: e.tensor_tensor(OT[nsl, hp, :], Osb[nsl, h, :], rden[nsl, :], ALU.mult),
                                                 reads=["Osb%d" % h, kr], writes=["OT%d_%d" % (hp, h)]))

                ld_wq(0)
                ld_wq(1)
                ld_wq(2)
                loads(0)
                qproj(0)
                loads(1)
                qproj(1)
                p.op("dve", lambda e: e.memset(PSB[:, :], 0.0), writes=kB)
                p.op("dve", lambda e: e.memset(PSD[:, :], 0.0), writes=kD)
                nbt = len(batches)
                last_of = {}
                last_it = {}
                for bi, b in enumerate(batches):
                    last_of[(b["hp"], b["h"])] = bi
                    last_it[b["it"]] = bi
                SKEW = 2
                for bi in range(nbt + SKEW):
                    if bi < nbt:
                        qk(bi)
                    if deferred:
                        deferred.pop(0)()
                    k = bi - SKEW
                    if k >= 0:
                        pb = batches[k]
                        is_last = last_of[(pb["hp"], pb["h"])] == k
                        pv(k, is_last)
                        if is_last:
                            evac_head(pb["hp"], pb["h"], pb["hp"] < 7)
                        if last_it[pb["it"]] == k and pb["it"] + 2 < len(its):
                            if pb["it"] + 3 < len(its):
                                ld_wq(pb["it"] + 3)
                            loads(pb["it"] + 2)
                            qproj(pb["it"] + 2)
                while deferred:
                    deferred.pop(0)()
                wpipe = LNPipe(0, 1, True)
                for i in range(NT):
                    def fo(e, i=i):
                        for half in range(2):
                            for cc in range(8):
                                ins = e.matmul(PSD[:, half * 512:(half + 1) * 512], lhsT=OT[:, cc, i * 128:(i + 1) * 128],
                                               rhs=Wo[:, cc, half * 512:(half + 1) * 512], start=(cc == 0), stop=(cc == 7))
                        return ins
                    p.op("pe", fo, reads=["OT%d_%d" % (hp, h) for hp in range(8) for h in range(2)] + ["Wo"], writes=kD)
                    wpipe.push(i, PSD[:, :], kD)
                wpipe.flush()
                p.phase_end()

        def run_unit(s, c):
            load_unit(s, c)
            for l in range(DEPTH):
                if l < 2:
                    pool_phase(l, c, True)
                else:
                    attn_phase(l - 2, c)
                if stop == "ln1_%d" % l:
                    return
                load_ln2(l)
                load_ln1((l + 1) % DEPTH)
                ffn_phase(l, c, l in (1, 2))
                if stop == "ln2_%d" % l:
                    return
                if l == 1:
                    kv_phase(c)
                    if stop == "kv":
                        return

        load_ln1(0)
        for s in range(nseq):
            for c in range(2):
                run_unit(s, c)
                store_unit(s, c)
                p.phase_end()
        p.finish()
        print("instructions emitted:", p.ninstr, "semaphores:", p.nsem)
    return nc


def host_constants():
    c = {}
    c["c_ident"] = np.eye(128, dtype=np.float32)
    pb = np.zeros((128, 12, 128), np.float32)
    s_ = np.arange(128)[:, None]
    t_ = np.arange(128)[None, :]
    for wi, w in enumerate(POOL_W):
        band = ((t_ - s_) >= 0) & ((t_ - s_) < w)
        eye = (s_ == t_).astype(np.float32)
        pb[:, wi * 3 + 0, :] = band.astype(np.float32) / w - eye
        pb[:, wi * 3 + 1, :] = ((t_ - s_ + 128) < w).astype(np.float32) / w
        cnt = np.minimum(t_ + 1, w).astype(np.float32)
        pb[:, wi * 3 + 2, :] = band.astype(np.float32) / cnt - eye
    c["c_poolB"] = pb
    cur = (s_ <= t_).astype(np.float32)
    prev = (s_ >= t_).astype(np.float32)
    m = np.zeros((128, 3, 512), np.float32)
    m[:, 0, :] = np.concatenate([cur, prev, cur, prev], axis=1)
    m[:, 1, :] = np.tile(cur[:, 0:64], (1, 8))
    m[:, 2, :] = np.tile(cur[:, 64:128], (1, 8))
    c["c_masks"] = m
    pidx = np.arange(128)
    km = np.zeros((128, 2), np.float32)
    for h in range(2):
        km[:, h] = (((pidx // 32) % 2) == h).astype(np.float32)
    c["c_km"] = km
    inv_freq = (np.float32(10000.0) ** (-(np.arange(0, 64, 2, dtype=np.float32)) / np.float32(64))).astype(np.float32)
    ang = (np.arange(S, dtype=np.float32)[:, None] * inv_freq[None, :]).astype(np.float32)
    cosv = np.cos(ang.astype(np.float64)).astype(np.float32).T
    sinv = np.sin(ang.astype(np.float64)).astype(np.float32).T
    i_of_p = pidx % 32
    a_of_p = pidx // 64
    c["c_cos"] = np.ascontiguousarray(cosv[i_of_p, :])
    sgn = np.where(a_of_p == 0, 1.0, -1.0).astype(np.float32)[:, None]
    c["c_sin"] = np.ascontiguousarray(sinv[i_of_p, :] * sgn)
    return c


def host_layouts(inp):
    o = {}
    f = lambda a: np.ascontiguousarray(a, dtype=np.float32)
    o["pw_l"] = f(inp["pool_w"].reshape(2, 4, 2, 128, 256).transpose(0, 3, 1, 2, 4))
    o["pool_scale"] = f(inp["pool_scale"])
    wq = inp["w_q"].reshape(2, 8, 128, 3, 8, 2, 2, 32)
    o["wq_l"] = f(wq.transpose(0, 3, 4, 2, 1, 6, 5, 7).reshape(2, 3, 8, 128, 8, 128))
    wk = inp["w_kv"][:, :3072].reshape(8, 128, 3, 8, 2, 2, 32)
    o["wk_l"] = f(wk.transpose(2, 3, 1, 0, 5, 4, 6).reshape(3, 8, 128, 8, 128))
    wv = inp["w_kv"][:, 3072:].reshape(8, 128, 3, 1024)
    o["wv_l"] = f(wv.transpose(2, 1, 0, 3))
    o["wo_l"] = f(inp["w_o"].reshape(2, 8, 128, 1024).transpose(0, 2, 1, 3))
    o["wg_l"] = f(inp["ffn_w_gate"].reshape(4, 8, 128, NJ, 128).transpose(0, 3, 2, 1, 4))
    o["wu_l"] = f(inp["ffn_w_up"].reshape(4, 8, 128, NJ, 128).transpose(0, 3, 2, 1, 4))
    o["wd_l"] = f(inp["ffn_w_down"].reshape(4, NJ, 128, 1024).transpose(0, 2, 1, 3))
    o["cw_l"] = f(inp["ffn_conv_w"].reshape(4, 3, NJ, 128).transpose(0, 3, 2, 1))
    o["cb_l"] = f(inp["ffn_conv_b"].reshape(4, NJ, 128).transpose(0, 2, 1))
    for n in ("ln1_g", "ln1_b", "ln2_g", "ln2_b"):
        o[n] = f(inp[n])
    return o


_CACHE = {}


def kernel(**inputs):
    x = np.ascontiguousarray(inputs["x"], dtype=np.float32)
    B = x.shape[0]
    per = B // N_CORES
    shared = host_layouts(inputs)
    shared.update(host_constants())
    if "nc" not in _CACHE:
        _CACHE["nc"] = build(nseq=per)
    nc = _CACHE["nc"]
    in_maps = []
    for ci in range(N_CORES):
        m = dict(shared)
        m["x"] = x[ci * per:(ci + 1) * per]
        in_maps.append(m)
    res = run_bass_kernel_spmd(nc, in_maps, core_ids=list(range(N_CORES)))
    out = np.concatenate([np.asarray(r["out"]) for r in res.results], axis=0)
    return out.astype(np.float32)
```
